# Optimizing a Trainium2 kernel written in Bass

```python
import jax, jax.numpy as jnp
from jax import lax
import numpy as np

D_MODEL = 1024
BATCH = 4
SEQ = 4096
DEPTH = 2

POOL_WIDTH = 512
POOL_WINDOWS = (2, 4, 8, 16)
N_POOL_GROUPS = len(POOL_WINDOWS)
POOL_GROUP = POOL_WIDTH // N_POOL_GROUPS
N_Q_HEADS = 8
N_KV_HEADS = 2
HEAD_DIM = 64
Q_GROUP = N_Q_HEADS // N_KV_HEADS
Q_WIDTH = N_Q_HEADS * HEAD_DIM
KV_WIDTH = N_KV_HEADS * HEAD_DIM
WINDOW = 128
BLOCK = 128
N_BRANCHES = 2
IN_WIDTH = POOL_WIDTH + Q_WIDTH + 2 * KV_WIDTH + N_BRANCHES * D_MODEL
D_FF = ((8 * D_MODEL + 3 * 256 - 1) // (3 * 256)) * 256
EPS = 1e-6

kernel_name = "hybrid_pool_swa_gated_encoder"


def rms_norm(x, gain):
    xf = x.astype(jnp.float32)
    y = xf * lax.rsqrt(jnp.mean(xf * xf, axis=-1, keepdims=True) + EPS)
    return (y * gain.astype(jnp.float32)).astype(x.dtype)


def alibi_slopes():
    h = jnp.arange(1, N_Q_HEADS + 1, dtype=jnp.float32)
    return jnp.exp2(-8.0 * h / N_Q_HEADS)


def multiscale_pool(z, w_group, scale):
    B, S, _ = z.shape
    zf = z.astype(jnp.float32)
    cs = jnp.pad(jnp.cumsum(zf, axis=1), ((0, 0), (1, 0), (0, 0)))
    t = jnp.arange(S)
    outs = []
    for g, w in enumerate(POOL_WINDOWS):
        sl = slice(g * POOL_GROUP, (g + 1) * POOL_GROUP)
        lo = jnp.clip(t - w // 2, 0, S)
        hi = jnp.clip(t + w // 2, 0, S)
        csg = cs[..., sl]
        count = (hi - lo).astype(jnp.float32)[None, :, None]
        mean = (csg[:, hi] - csg[:, lo]) / count
        outs.append(mean - zf[..., sl])
    d = jnp.stack(outs, axis=2).astype(z.dtype)
    y = jnp.einsum("bsgc,gcd->bsgd", d, w_group).reshape(B, S, POOL_WIDTH)
    return y * scale


def band_blocks(t, nb):
    B = t.shape[0]
    t = jnp.pad(t, ((0, 0), (BLOCK, BLOCK), (0, 0), (0, 0)))
    t = t.reshape(B, nb + 2, BLOCK, N_KV_HEADS, HEAD_DIM)
    return jnp.concatenate([t[:, :-2], t[:, 1:-1], t[:, 2:]], axis=2)


def windowed_gqa(q, k, v, sink):
    B, S, _ = q.shape
    nb = S // BLOCK
    qb = (q * (HEAD_DIM ** -0.5)).reshape(B, nb, BLOCK, N_KV_HEADS, Q_GROUP, HEAD_DIM)
    kb = band_blocks(k.reshape(B, S, N_KV_HEADS, HEAD_DIM), nb)
    vb = band_blocks(v.reshape(B, S, N_KV_HEADS, HEAD_DIM), nb)
    scores = jnp.einsum("bnqhgd,bnkhd->bnhgqk", qb, kb).astype(jnp.float32)
    a = jnp.arange(BLOCK)[:, None]
    c = jnp.arange(3 * BLOCK)[None, :]
    absdist = jnp.abs(a - c + BLOCK)
    kpos = (jnp.arange(nb)[:, None] - 1) * BLOCK + jnp.arange(3 * BLOCK)[None, :]
    kvalid = (kpos >= 0) & (kpos < S)
    mask = (absdist <= WINDOW)[None, :, :] & kvalid[:, None, :]
    slopes = alibi_slopes().reshape(N_KV_HEADS, Q_GROUP)[:, :, None, None]
    scores = scores - slopes * absdist.astype(jnp.float32)
    scores = jnp.where(mask[None, :, None, None, :, :], scores, -jnp.inf)
    sink_l = sink.astype(jnp.float32).reshape(N_KV_HEADS, Q_GROUP)[:, :, None, None]
    m = jnp.maximum(jnp.max(scores, axis=-1, keepdims=True), sink_l)
    p = jnp.exp(scores - m)
    denom = jnp.sum(p, axis=-1, keepdims=True) + jnp.exp(sink_l - m)
    p = (p / denom).astype(v.dtype)
    out = jnp.einsum("bnhgqk,bnkhd->bnqhgd", p, vb)
    return out.reshape(B, S, Q_WIDTH)


def setup_inputs(seed: int = 0) -> dict:
    key = jax.random.key(seed)
    ks = jax.random.split(key, 16)
    f32 = jnp.float32

    def dense(k, shape, fan_in):
        return jax.random.normal(k, shape, f32) * (fan_in ** -0.5)

    def gain(k, shape):
        return 1.0 + 0.05 * jax.random.normal(k, shape, f32)

    return {
        "x": jax.random.normal(ks[0], (BATCH, SEQ, D_MODEL), f32),
        "norm_mix": gain(ks[1], (DEPTH, D_MODEL)),
        "w_in": dense(ks[2], (DEPTH, D_MODEL, IN_WIDTH), D_MODEL),
        "w_pool_group": dense(ks[3], (DEPTH, N_POOL_GROUPS, POOL_GROUP, POOL_GROUP), POOL_GROUP),
        "pool_scale": gain(ks[4], (DEPTH, POOL_WIDTH)),
        "sink": 0.5 * jax.random.normal(ks[5], (DEPTH, N_Q_HEADS), f32),
        "w_pool_branch": dense(ks[6], (DEPTH, POOL_WIDTH, D_MODEL), POOL_WIDTH),
        "w_attn_branch": dense(ks[7], (DEPTH, Q_WIDTH, D_MODEL), Q_WIDTH),
        "w_out": dense(ks[8], (DEPTH, D_MODEL, D_MODEL), D_MODEL),
        "norm_ffn": gain(ks[9], (DEPTH, D_MODEL)),
        "w_ffn_gate": dense(ks[10], (DEPTH, D_MODEL, D_FF), D_MODEL),
        "w_ffn_up": dense(ks[11], (DEPTH, D_MODEL, D_FF), D_MODEL),
        "w_ffn_down": dense(ks[12], (DEPTH, D_FF, D_MODEL), D_FF),
        "norm_final": gain(ks[13], (D_MODEL,)),
    }


def reference(x, norm_mix, w_in, w_pool_group, pool_scale, sink, w_pool_branch,
              w_attn_branch, w_out, norm_ffn, w_ffn_gate, w_ffn_up, w_ffn_down, norm_final):
    splits = [POOL_WIDTH,
              POOL_WIDTH + Q_WIDTH,
              POOL_WIDTH + Q_WIDTH + KV_WIDTH,
              POOL_WIDTH + Q_WIDTH + 2 * KV_WIDTH,
              POOL_WIDTH + Q_WIDTH + 2 * KV_WIDTH + D_MODEL]
    h = x
    for l in range(DEPTH):
        u = rms_norm(h, norm_mix[l])
        proj = u @ w_in[l]
        z_pool, q, k, v, g_pool, g_attn = jnp.split(proj, splits, axis=-1)
        y_pool = multiscale_pool(z_pool, w_pool_group[l], pool_scale[l]) @ w_pool_branch[l]
        y_attn = windowed_gqa(q, k, v, sink[l]) @ w_attn_branch[l]
        merged = jax.nn.sigmoid(g_pool) * y_pool + jax.nn.sigmoid(g_attn) * y_attn
        h = h + merged @ w_out[l]
        u = rms_norm(h, norm_ffn[l])
        h = h + (jax.nn.silu(u @ w_ffn_gate[l]) * (u @ w_ffn_up[l])) @ w_ffn_down[l]
    return rms_norm(h, norm_final)
```

```python
import numpy as np
from contextlib import ExitStack
import concourse.bass as bass
import concourse.mybir as mybir
from concourse.bass_utils import run_bass_kernel_spmd

F32 = mybir.dt.float32
BF16 = mybir.dt.bfloat16
AF = mybir.ActivationFunctionType
ALU = mybir.AluOpType

D = 1024
SEQ = 4096
NB = 4
DEPTH = 2
DFF = 2816
INW = 3328
TT = 1536
OWN = 1024
HALO = 256
NU = 2
NCORES = 8
EPS = 1e-6
POOLW = (2, 4, 8, 16)
ARENA = 32768
FCH = (5, 5, 4, 4, 4)
NEG = -30000.0
_LIMIT = 999
_NU_RUN = NU
_P1 = 31


class _Op:
    __slots__ = ("eng", "fn", "fns", "sem", "deps", "needed", "ticket", "waits", "stream")


class Plan:
    def __init__(self):
        self.ops = []
        self.lists = {n: [] for n in ("pe", "act", "dve", "pool", "sp")}
        self.last_w = {}
        self.readers = {}

    def _add(self, op, reads, writes):
        oid = len(self.ops)
        deps = {}
        def dep(d):
            s = self.ops[d].stream
            if deps.get(s, -1) < d:
                deps[s] = d
        for k in reads:
            d = self.last_w.get(k)
            if d is not None:
                dep(d)
        for k in writes:
            d = self.last_w.get(k)
            if d is not None:
                dep(d)
            for d in self.readers.get(k, {}).values():
                dep(d)
        for k in writes:
            self.last_w[k] = oid
            self.readers[k] = {}
        for k in reads:
            self.readers.setdefault(k, {})[op.stream] = oid
        if op.eng == "pe":
            deps.pop("pe", None)
        op.deps = list(deps.values())
        op.needed = False
        self.ops.append(op)
        self.lists[op.eng].append(op)
        return oid

    def op(self, eng, fn, reads=(), writes=(), free=()):
        o = _Op()
        o.eng = eng; o.fn = fn; o.fns = None; o.sem = None; o.stream = eng
        r = self._add(o, reads, writes)
        for b in free:
            self.free_hook(b)
        return r

    def dma(self, eng, sem, fns, reads=(), writes=()):
        o = _Op()
        o.eng = eng; o.fn = None; o.fns = list(fns); o.sem = sem; o.stream = sem
        return self._add(o, reads, writes)

    def finalize(self):
        for o in self.ops:
            for d in o.deps:
                self.ops[d].needed = True
        cnt = {}
        for o in self.ops:
            if o.fns is not None:
                cnt[o.sem] = cnt.get(o.sem, 0) + 16 * len(o.fns)
                o.ticket = cnt[o.sem]
            elif o.needed:
                cnt[o.eng] = cnt.get(o.eng, 0) + 1
                o.ticket = cnt[o.eng]
        for o in self.ops:
            o.waits = [(self.ops[d].stream, self.ops[d].ticket) for d in o.deps]
        self.totals = cnt

    def replay(self, name, e, sems):
        waited = {}
        for o in self.lists[name]:
            for (s, v) in o.waits:
                if waited.get(s, 0) < v:
                    e.wait_ge(sems[s], v)
                    waited[s] = v
            if o.fns is not None:
                for f in o.fns:
                    f(e).then_inc(sems[o.sem], 16)
            else:
                ins = o.fn(e)
                if o.needed:
                    ins.then_inc(sems[name], 1)


def _blocks(t0, n):
    return range(t0 // 128, (t0 + n - 1) // 128 + 1)


def build_nc():
    nc = bass.Bass("TRN2", target_bir_lowering=False)
    dt = nc.dram_tensor
    xT_d = dt("xT", [NU, D, TT], F32, kind="ExternalInput").ap()
    hm_d = dt("hmask", [NU, 2], F32, kind="ExternalInput").ap()
    kbias_d = dt("kbias", [NU, 128, 12], F32, kind="ExternalInput").ap()
    corr_d = dt("corr", [NU, 64], F32, kind="ExternalInput").ap()
    biasT_d = dt("biasT", [128, 3072], F32, kind="ExternalInput").ap()
    ident_d = dt("ident", [128, 128], F32, kind="ExternalInput").ap()
    norm_mix_d = dt("norm_mix", [128, DEPTH * 8], F32, kind="ExternalInput").ap()
    w_in_d = dt("w_in", [DEPTH, D, INW], F32, kind="ExternalInput").ap()
    wpg_d = dt("w_pool_group", [DEPTH, 4, 128, 128], F32, kind="ExternalInput").ap()
    pscale_d = dt("pool_scale", [128, DEPTH * 4], F32, kind="ExternalInput").ap()
    sink_d = dt("sink", [DEPTH * 8], F32, kind="ExternalInput").ap()
    wpb_d = dt("w_pool_branch", [DEPTH, 512, D], F32, kind="ExternalInput").ap()
    wab_d = dt("w_attn_branch", [DEPTH, 512, D], F32, kind="ExternalInput").ap()
    wout_d = dt("w_out", [DEPTH, D, D], F32, kind="ExternalInput").ap()
    norm_ffn_d = dt("norm_ffn", [128, DEPTH * 8], F32, kind="ExternalInput").ap()
    wg_d = dt("w_ffn_gate", [DEPTH, D, DFF], F32, kind="ExternalInput").ap()
    wu_d = dt("w_ffn_up", [DEPTH, D, DFF], F32, kind="ExternalInput").ap()
    wd_d = dt("w_ffn_down", [DEPTH, DFF, D], F32, kind="ExternalInput").ap()
    norm_final_d = dt("norm_final", [128, 8], F32, kind="ExternalInput").ap()
    out_d = dt("out", [NU, D, OWN], F32, kind="ExternalOutput").ap()

    P = Plan()
    with ExitStack() as es:
        sb = lambda name, shape, dty: es.enter_context(nc.sbuf_tensor(name, shape, dty))
        hT = sb("hT", [128, 8, TT], F32)
        mid = sb("mid", [128, 4, 8, 512], BF16)
        kT = sb("kT", [128, 1, TT], BF16)
        vv = sb("vv", [128, 12, 128], BF16)
        zT = sb("zT", [128, 4, TT], BF16)
        hm = sb("hm", [128, 2], F32)
        ub = sb("ub", [128, 8, 512], BF16)
        sq = sb("sq", [128, 2, 512], BF16)
        arena = sb("arena", [128, ARENA], BF16)
        wk = sb("wk", [128, 9792], BF16)
        biasT = sb("biasTs", [128, 6, 512], BF16)
        ident = sb("idents", [128, 128], BF16)
        ones = sb("ones", [128, 128], BF16)
        gmix = sb("gmix", [128, DEPTH, 8], F32)
        gffn = sb("gffn", [128, DEPTH, 8], F32)
        gfin = sb("gfin", [128, 8], F32)
        pscale = sb("pscale", [128, DEPTH, 4], F32)
        esk = sb("esk", [128, 16], F32)
        esrow = [sb("esrow0", [128, 512], BF16), sb("esrow1", [128, 512], BF16)]
        kbias = sb("kbiass", [128, 12], F32)
        corr = sb("corrs", [128, 64], F32)
        epsb = sb("epsb", [128, 1], F32)
        qsc = sb("qsc", [128, 2], F32)
        rden_t = sb("rden", [128, 2, 512], BF16)
        PT2 = sb("PT2", [128, 3, 512], BF16)
        ps = es.enter_context(nc.psum_tensor("ps", [128, 8, 512], F32))
        sem_names = ["pe", "act", "dve", "pool", "w0", "w1", "inp0", "inp1", "inp2", "inq0", "inq1", "inq2", "inc", "inm", "cb", "outp0", "outp1"]
        sems = {n: es.enter_context(nc.semaphore("s_" + n)) for n in sem_names}

        def wkv(start, length, dty=BF16):
            v = wk[:, start:start + length]
            return v.bitcast(dty) if dty != BF16 else v
        def wkk(start, length):
            return [("wk", b) for b in range(start // 32, (start + length - 1) // 32 + 1)]
        qe = wkv(0, 2048).rearrange("p (c t) -> p c t", c=4)
        qo = wkv(2048, 2048).rearrange("p (c t) -> p c t", c=4)
        qb_k = wkk(0, 4096)
        dg = wkv(4096, 2048).rearrange("p (c t) -> p c t", c=4)
        def dg_k(g): return wkk(4096 + 512 * g, 512)
        PT = wkv(8256, 1536).rearrange("p (c t) -> p c t", c=3)
        def PT_k(kb): return wkk(8256 + 512 * kb, 512)
        pa = wkv(6144, 1056, F32); pa_k = wkk(6144, 1056)
        pb = wkv(7200, 1056, F32); pb_k = wkk(7200, 1056)
        sgp = wkv(0, 1024, F32); sgp_k = wkk(0, 1024)
        sga = wkv(1024, 1024, F32); sga_k = wkk(1024, 1024)
        t1 = wkv(2048, 1024, F32); t1_k = wkk(2048, 1024)
        t2 = wkv(3072, 1024, F32); t2_k = wkk(3072, 1024)
        sgt = [wkv(0, 1024, F32), wkv(1024, 1024, F32)]
        sgt_k = [wkk(0, 1024), wkk(1024, 1024)]
        actb = wkv(2048, 2560).rearrange("p (c t) -> p c t", c=5)
        def actb_k(j): return wkk(2048 + 512 * j, 512)
        yout = [wkv(4608, 1024, F32), wkv(5632, 1024, F32)]
        yout_k = [wkk(4608, 1024), wkk(5632, 1024)]

        bank_ctr = [0]
        free_banks = list(range(7))
        def nbank():
            assert free_banks, "out of PSUM banks"
            return free_banks.pop(0)
        def bfree(b):
            assert b not in free_banks and 0 <= b < 7
            free_banks.append(b)
        P.free_hook = bfree
        SB = 7

        bg = []
        def bg_add(steps, tag=None):
            bg.extend((tag, f) for f in steps)
        def bg_step(k=1):
            for _ in range(k):
                if bg:
                    bg.pop(0)[1]()
        def bg_flush():
            while bg:
                bg.pop(0)[1]()
        bgp = []
        def bgp_step(k=1):
            for _ in range(k):
                if bgp:
                    bgp.pop(0)()
        def bg_require(tag):
            while any(t == tag for t, _ in bg):
                bg.pop(0)[1]()

        def hk(cs, t0, n):
            return [("h", c, b) for c in cs for b in _blocks(t0, n)]

        def load_consts():
            P.dma("sp", "inc", [
                lambda e: e.dma_start(out=gmix[:], in_=norm_mix_d.rearrange("p (l c) -> p l c", l=DEPTH)),
                lambda e: e.dma_start(out=gffn[:], in_=norm_ffn_d.rearrange("p (l c) -> p l c", l=DEPTH)),
                lambda e: e.dma_start(out=gfin[:], in_=norm_final_d),
                lambda e: e.dma_start(out=pscale[:], in_=pscale_d.rearrange("p (l c) -> p l c", l=DEPTH)),
                lambda e: e.dma_start(out=esk[:], in_=sink_d.partition_broadcast(128)),
            ], writes=[("const",), ("esk",)])
            P.dma("pool", "cb", [
                lambda e: e.dma_start(out=biasT[:], in_=biasT_d.rearrange("p (c t) -> p c t", c=6)),
                lambda e: e.dma_start(out=ident[:], in_=ident_d),
            ], writes=[("cb",)])
            P.op("act", lambda e: e.activation(out=esk[:], in_=esk[:], func=AF.Exp), reads=[("esk",)], writes=[("esk",)])
            for l_ in range(DEPTH):
                for hh_ in range(2):
                    for g_ in range(4):
                        p0_ = 32 * hh_
                        ix_ = l_ * 8 + 4 * hh_ + g_
                        P.op("dve", lambda e, l_=l_, p0_=p0_, g_=g_, ix_=ix_: e.tensor_copy(
                            out=esrow[l_][p0_:p0_ + 1, g_ * 128:(g_ + 1) * 128], in_=esk[p0_:p0_ + 1, ix_:ix_ + 1].to_broadcast([1, 128])),
                            reads=[("esk",)], writes=[("esrow",)])


        phase_ctr = [0]

        def wview(off, shape3):
            n = shape3[0] * shape3[1]
            return arena[:, off:off + n].rearrange("p (c f) -> p c f", c=shape3[0])

        def colsrc(w2d, a, b):
            return w2d[:, a:b].rearrange("(c p) f -> p c f", p=128)

        def rowsrc(w2d, a, b):
            return w2d[a:b, :].rearrange("(c p) f -> p c f", p=128)

        def mk_phase(kind, l, jlo=0, nj=0):
            if kind == "P0":
                size = 6 * 1024
                def views(base):
                    return dict(kd=wview(base, (8, 128)), vd=wview(base + 1024, (8, 128)), wz=wview(base + 2048, (8, 512)))
                def fns(base):
                    v = views(base)
                    return [lambda e: e.dma_start(out=v["kd"], in_=colsrc(w_in_d[l], 1024, 1152)),
                            lambda e: e.dma_start(out=v["vd"], in_=colsrc(w_in_d[l], 1152, 1280)),
                            lambda e: e.dma_start(out=v["wz"], in_=colsrc(w_in_d[l], 0, 512))]
            elif kind == "P1":
                size = 4096 + 512
                def views(base):
                    return dict(wq=wview(base, (8, 512)), wpg=wview(base + 4096, (4, 128)))
                def fns(base):
                    v = views(base)
                    wqv = v["wq"].rearrange("p k (g h d) -> p k g h d", g=4, h=2)
                    out = []
                    for h_ in range(2):
                        for g_ in range(4):
                            c0 = 512 + 256 * h_ + 64 * g_
                            out.append(lambda e, h_=h_, g_=g_, c0=c0: e.dma_start(out=wqv[:, :, g_, h_, :], in_=colsrc(w_in_d[l], c0, c0 + 64)))
                    out.append(lambda e: e.dma_start(out=v["wpg"], in_=wpg_d[l].rearrange("g p f -> p g f")))
                    return out
            elif kind == "P2a":
                size = 24576
                def views(base):
                    return dict(gp=wview(base, (8, 1024)), ga=wview(base + 8192, (8, 1024)),
                                wpb=wview(base + 16384, (4, 1024)), wab=wview(base + 20480, (4, 1024)))
                def fns(base):
                    v = views(base)
                    return [lambda e: e.dma_start(out=v["gp"], in_=colsrc(w_in_d[l], 1280, 2304)),
                            lambda e: e.dma_start(out=v["wpb"], in_=rowsrc(wpb_d[l], 0, 512)),
                            lambda e: e.dma_start(out=v["ga"], in_=colsrc(w_in_d[l], 2304, 3328)),
                            lambda e: e.dma_start(out=v["wab"][0:64, :, :], in_=wab_d[l][0:256, :].rearrange("(g d) f -> d g f", d=64)),
                            lambda e: e.dma_start(out=v["wab"][64:128, :, :], in_=wab_d[l][256:512, :].rearrange("(g d) f -> d g f", d=64))]
            elif kind == "P2b":
                size = 8192
                def views(base):
                    return dict(wo=wview(base, (8, 1024)))
                def fns(base):
                    v = views(base)
                    return [lambda e: e.dma_start(out=v["wo"], in_=colsrc(wout_d[l], 0, 1024))]
            elif kind == "F":
                size = nj * 3072
                def views(base):
                    return dict(wg=wview(base, (8, 128 * nj)), wu=wview(base + 1024 * nj, (8, 128 * nj)),
                                wd=wview(base + 2048 * nj, (nj, 1024)))
                def fns(base):
                    v = views(base)
                    a, b = 128 * jlo, 128 * (jlo + nj)
                    return [lambda e: e.dma_start(out=v["wg"], in_=colsrc(wg_d[l], a, b)),
                            lambda e: e.dma_start(out=v["wu"], in_=colsrc(wu_d[l], a, b)),
                            lambda e: e.dma_start(out=v["wd"], in_=rowsrc(wd_d[l], a, b))]
            return dict(kind=kind, l=l, size=size, fns=fns, views=views, jlo=jlo, nj=nj)

        phases = []
        for u in range(NU):
            for l in range(DEPTH):
                phases += [mk_phase("P0", l), mk_phase("P1", l), mk_phase("P2a", l), mk_phase("P2b", l)]
                j = 0
                for nj in FCH:
                    phases.append(mk_phase("F", l, j, nj))
                    j += nj
        for i, ph in enumerate(phases):
            ph["par"] = i % 2
            ph["base"] = 0 if i % 2 == 0 else ARENA - ph["size"]
            ph["u"] = i // (len(phases) // NU)
        for i in range(len(phases) - 1):
            assert phases[i]["size"] + phases[i + 1]["size"] <= ARENA

        def issue_weights(i):
            if i >= len(phases):
                return
            ph = phases[i]
            P.dma("pool", "w%d" % ph["par"], ph["fns"](ph["base"]), writes=[("w", ph["par"])])

        def norm_steps(t0, n, gain, dst, dst_keys):
            b = SB
            steps = []
            def s_sq(k):
                def f():
                    P.op("act", lambda e: e.activation(out=sq[:, k % 2, :n], in_=hT[:, k, t0:t0 + n], func=AF.Square),
                         reads=hk([k], t0, n), writes=[("sq", k % 2)])
                    P.op("pe", lambda e: e.matmul(ps[:, b, :n], lhsT=ones[:, :], rhs=sq[:, k % 2, :n], start=(k == 0), stop=(k == 7)),
                         reads=[("sq", k % 2), ("ones",)], writes=[("ps", b)])
                return f
            def s_rs():
                P.op("act", lambda e: e.activation(out=ps[:, b, :n], in_=ps[:, b, :n], func=AF.Ln, bias=epsb[:], scale=1.0 / D),
                     reads=[("ps", b), ("eps",)], writes=[("ps", b)])
                P.op("act", lambda e: e.activation(out=ps[:, b, :n], in_=ps[:, b, :n], func=AF.Exp, scale=-0.5), reads=[("ps", b)], writes=[("ps", b)])
            def s_sc(k):
                def f():
                    P.op("dve", lambda e: e.scalar_tensor_tensor(out=dst(k), in0=hT[:, k, t0:t0 + n], scalar=gain[:, k:k + 1],
                                                                  in1=ps[:, b, :n], op0=ALU.mult, op1=ALU.mult),
                         reads=hk([k], t0, n) + [("ps", b), ("const",)], writes=dst_keys(k))
                return f
            for k in range(8):
                steps.append(s_sq(k))
            steps.append(s_rs)
            for k in range(8):
                steps.append(s_sc(k))
            return steps

        def norm(t0, n, gain, dst, dst_keys):
            for f in norm_steps(t0, n, gain, dst, dst_keys):
                f()

        zflat = zT[:].rearrange("p g t -> p (g t)")[:, 0:4096].rearrange("p (k t) -> p k t", k=8)
        Z_ALL = [("z", g, bb) for g in range(4) for bb in range(12)]
        def ubuf(kind, n):
            if kind == "A":
                return (lambda k: ub[:, k, :n]), (lambda k: [("u", k)]), [("u", k) for k in range(8)]
            if kind == "S":
                return (lambda k: mid[:, 3, k, :n]), (lambda k: [("mid", 3, k)]), [("mid", 3, k) for k in range(8)]
            return (lambda k: zflat[:, k, :n]), (lambda k: Z_ALL), Z_ALL

        BGN = [1]
        def proj(wv, col0, n, par, rhs_fn, rhs_keys, nk=8):
            b = nbank()
            def f(e):
                for k in range(nk):
                    ins = e.matmul(ps[:, b, :n], lhsT=wv[:, k, col0:col0 + 128], rhs=rhs_fn(k), start=(k == 0), stop=(k == nk - 1))
                return ins
            P.op("pe", f, reads=[("w", par)] + rhs_keys, writes=[("ps", b)])
            bg_step(BGN[0])
            return b

        def unit_inputs(u):
            for part in range(3):
                a0 = 512 * part
                P.dma("sp" if u == 0 else "pool", ("inp%d" if u == 0 else "inq%d") % part, [(lambda e, c=c, a0=a0: e.dma_start(out=hT[:, c, a0:a0 + 512], in_=xT_d[u, c * 128:(c + 1) * 128, a0:a0 + 512]))
                                             for c in range(8)], writes=hk(range(8), a0, 512))
                if part == 0:
                    if u == 0:
                        load_consts()
                    small_inputs(u)

        def small_inputs(u):
            P.dma("sp", "inm", [
                lambda e: e.dma_start(out=kbias[:], in_=kbias_d[u]),
                lambda e: e.dma_start(out=corr[:], in_=corr_d[u].partition_broadcast(128)),
                lambda e: e.dma_start(out=hm[:], in_=hm_d[u].partition_broadcast(128)),
            ], writes=[("kbias",), ("corr",), ("hm",)])

        def p0_tile(l, t0, n, W, par, U):
            uf, _, UK = U
            b = proj(W["kd"], 0, n, par, uf, UK)
            P.op("act", lambda e, b=b: e.activation(out=kT[:, 0, t0:t0 + n], in_=ps[:, b, :n], func=AF.Copy),
                 reads=[("ps", b)], writes=[("k", 0, bb) for bb in _blocks(t0, n)], free=[b])
            for g in range(4):
                b = proj(W["wz"], g * 128, n, par, uf, UK)
                P.op("dve", lambda e, b=b, g=g: e.tensor_copy(out=zT[:, g, t0:t0 + n], in_=ps[:, b, :n]),
                     reads=[("ps", b)], writes=[("z", g, bb) for bb in _blocks(t0, n)], free=[b])
            for i in range(n // 128):
                b = nbank()
                gb = t0 // 128 + i
                def f(e, b=b, i=i):
                    for k in range(8):
                        ins = e.matmul(ps[:, b, 0:128], lhsT=uf(k)[:, i * 128:(i + 1) * 128], rhs=W["vd"][:, k, :], start=(k == 0), stop=(k == 7))
                    return ins
                P.op("pe", f, reads=[("w", par)] + UK, writes=[("ps", b)])
                P.op("act", lambda e, b=b, gb=gb: e.activation(out=vv[:, gb, 0:128], in_=ps[:, b, 0:128], func=AF.Copy),
                     reads=[("ps", b)], writes=[("v", gb)], free=[b])
                bg_step(2)

        def pool_steps(l, slot, t0, n, W, par):
            groups = []
            tails = []
            for g, w in enumerate(POOLW):
                steps = []
                base_t = t0 - w // 2
                L = n + w - 1
                zk = [("z", g, bb) for bb in _blocks(base_t, L)]
                cur = None; curk = None
                bufs = [(pa, pa_k), (pb, pb_k)]
                s_ = 1
                bi = 0
                while s_ < w:
                    Ln = L - s_
                    ob, obk = bufs[bi]
                    if cur is None:
                        steps.append(lambda ob=ob, obk=obk, Ln=Ln, s_=s_, base_t=base_t, g=g, zk=zk: P.op("pool", lambda e: e.tensor_tensor(
                            out=ob[:, :Ln], in0=zT[:, g, base_t:base_t + Ln], in1=zT[:, g, base_t + s_:base_t + s_ + Ln], op=ALU.add),
                            reads=zk, writes=obk))
                    else:
                        steps.append(lambda ob=ob, obk=obk, cur=cur, curk=curk, Ln=Ln, s_=s_: P.op("pool", lambda e: e.tensor_tensor(
                            out=ob[:, :Ln], in0=cur[:, :Ln], in1=cur[:, s_:s_ + Ln], op=ALU.add),
                            reads=curk, writes=obk))
                    cur, curk = ob, obk
                    L = Ln
                    s_ *= 2
                    bi ^= 1
                for side, lo in ((0, 256), (1, 1272)):
                    a0 = max(lo, t0); a1 = min(lo + 8, t0 + n)
                    if a1 > a0:
                        cc = side * 32 + g * 8 + (a0 - lo)
                        steps.append(lambda cur=cur, curk=curk, a0=a0, a1=a1, cc=cc: P.op("pool", lambda e: e.tensor_tensor(
                            out=cur[:, a0 - t0:a1 - t0], in0=cur[:, a0 - t0:a1 - t0], in1=corr[:, cc:cc + (a1 - a0)], op=ALU.mult),
                            reads=curk + [("corr",)], writes=curk))
                def fin_a(cur=cur, curk=curk, g=g, w=w):
                    P.op("dve", lambda e: e.scalar_tensor_tensor(
                        out=dg[:, g, :n], in0=cur[:, :n], scalar=1.0 / w, in1=zT[:, g, t0:t0 + n], op0=ALU.mult, op1=ALU.subtract),
                        reads=curk + [("z", g, bb) for bb in _blocks(t0, n)], writes=dg_k(g))
                steps.append(fin_a)
                def fin_b(g=g):
                    b = nbank()
                    P.op("pe", lambda e: e.matmul(ps[:, b, :n], lhsT=W["wpg"][:, g, :], rhs=dg[:, g, :n], start=True, stop=True),
                         reads=[("w", par)] + dg_k(g), writes=[("ps", b)])
                    P.op("dve", lambda e: e.tensor_scalar(out=mid[:, slot, g, :n], in0=ps[:, b, :n], scalar1=pscale[:, l, g:g + 1],
                                                           scalar2=None, op0=ALU.mult),
                         reads=[("ps", b), ("const",)], writes=[("mid", slot, g)], free=[b])
                groups.append(lambda steps=steps: [f() for f in steps])
                tails.append(fin_b)
            return groups + tails

        def p1_tile(l, ti, t0, n, W, par, U, z_hazard=False):
            uf, _, UK = U
            slot = ti
            if z_hazard:
                BGN[0] = 2
            for c in range(4):
                b = proj(W["wq"], c * 128, n, par, uf, UK)
                P.op("dve", lambda e, b=b, c=c: e.tensor_scalar(out=qe[:, c, :n], in0=ps[:, b, :n], scalar1=qsc[:, 0:1], scalar2=None, op0=ALU.mult),
                     reads=[("ps", b), ("qsc",)], writes=wkk(512 * c, 512))
                P.op("dve", lambda e, b=b, c=c: e.tensor_scalar(out=qo[:, c, :n], in0=ps[:, b, :n], scalar1=qsc[:, 1:2], scalar2=None, op0=ALU.mult),
                     reads=[("ps", b), ("qsc",)], writes=wkk(2048 + 512 * c, 512), free=[b])
            bgp.extend(pool_steps(l, slot, t0, n, W, par))
            if z_hazard:
                bgp_step(16)
                BGN[0] = 3
            items = [(i, hh) for i in range(n // 128) for hh in range(2)]
            def stage_a(it):
                i, hh = it
                a = t0 + 128 * i
                bss = []
                for kb in range(3):
                    gb = a // 128 - 1 + kb
                    bs = nbank()
                    bss.append(bs)
                    def fst(e, bs=bs, kb=kb, gb=gb):
                        e.matmul(ps[:, bs, :], lhsT=ident[:, :], rhs=biasT[:, hh * 3 + kb, :], start=True, stop=False, skip_group_check=True)
                        src = qe if hh == 0 else qo
                        for g in range(4):
                            ins = e.matmul(ps[:, bs, g * 128:(g + 1) * 128], lhsT=kT[:, 0, gb * 128:(gb + 1) * 128],
                                           rhs=src[:, g, i * 128:(i + 1) * 128], start=False, stop=True, skip_group_check=True)
                        return ins
                    P.op("pe", fst, reads=[("cb",), ("k", 0, gb)] + qb_k, writes=[("ps", bs)])
                return bss
            def stage_b(it, bss, jj):
                i, hh = it
                PTb = PT if jj % 2 == 0 else PT2
                PTk = (lambda kb: PT_k(kb)) if jj % 2 == 0 else (lambda kb: [("PT2", kb)])
                a = t0 + 128 * i
                bo = nbank(); bd = nbank()
                for kb in range(3):
                    gb = a // 128 - 1 + kb
                    bs = bss[kb]
                    P.op("act", lambda e, bs=bs, kb=kb, gb=gb: e.activation(out=PTb[:, kb, :], in_=ps[:, bs, :], func=AF.Exp,
                                                                             bias=kbias[:, gb:gb + 1], scale=1.0),
                         reads=[("ps", bs), ("kbias",)], writes=PTk(kb), free=[bs])
                    def fpv(e, kb=kb, gb=gb):
                        e.matmul(ps[:, bo, :], lhsT=vv[:, gb, 0:128], rhs=PTb[:, kb, :], start=(kb == 0), stop=(kb == 2))
                        ins = e.matmul(ps[:, bd, :], lhsT=ones[:, :], rhs=PTb[:, kb, :], start=(kb == 0), stop=False)
                        if kb == 2:
                            p0 = 32 * hh
                            ins = e.matmul(ps[:, bd, :], lhsT=ones[p0:p0 + 1, :], rhs=esrow[l][p0:p0 + 1, :], start=False, stop=True)
                        return ins
                    P.op("pe", fpv, reads=[("v", gb), ("ones",), ("esrow",)] + PTk(kb), writes=[("ps", bo), ("ps", bd)])
                return bo, bd
            def stage_c(it, bo, bd, jj):
                i, hh = it
                rden = rden_t[:, jj % 2, :]
                rden_k = [("rden", jj % 2)]
                P.op("act", lambda e: e.activation(out=ps[:, bd, :], in_=ps[:, bd, :], func=AF.Ln), reads=[("ps", bd)], writes=[("ps", bd)])
                P.op("act", lambda e: e.activation(out=rden, in_=ps[:, bd, :], func=AF.Exp, scale=-1.0), reads=[("ps", bd)], writes=rden_k, free=[bd])
                psv = ps[:, bo, :].rearrange("p (g q) -> p g q", g=4)
                rdv = rden.rearrange("p (g q) -> p g q", g=4)
                r0 = 64 * hh
                P.op("dve", lambda e: e.tensor_tensor(
                    out=mid[r0:r0 + 64, slot, 4:8, i * 128:(i + 1) * 128], in0=psv[r0:r0 + 64, :, :], in1=rdv[r0:r0 + 64, :, :], op=ALU.mult),
                    reads=[("ps", bo)] + rden_k, writes=[("mid", slot, 4 + g_) for g_ in range(4)], free=[bo])
            A = {0: stage_a(items[0])}
            Bk = {0: stage_b(items[0], A[0], 0)}
            if len(items) > 1:
                A[1] = stage_a(items[1])
            for j in range(len(items)):
                if j + 1 < len(items):
                    Bk[j + 1] = stage_b(items[j + 1], A[j + 1], j + 1)
                if j + 2 < len(items):
                    A[j + 2] = stage_a(items[j + 2])
                stage_c(items[j], *Bk[j], j)
                bgp_step((8 + len(items) - 1) // len(items))
                bg_step(5)
            bgp_step(16)

        def p2a_tile(ti, t0, n, ms, W, par, U):
            uf, _, UK = U
            slot = ti
            for c in range(8):
                b1 = proj(W["gp"], c * 128, n, par, uf, UK)
                P.op("act", lambda e, b1=b1: e.activation(out=sgp[:, :n], in_=ps[:, b1, :n], func=AF.Sigmoid),
                     reads=[("ps", b1)], writes=sgp_k, free=[b1])
                b2 = proj(W["ga"], c * 128, n, par, uf, UK)
                P.op("act", lambda e, b2=b2: e.activation(out=sga[:, :n], in_=ps[:, b2, :n], func=AF.Sigmoid),
                     reads=[("ps", b2)], writes=sga_k, free=[b2])
                b3 = proj(W["wpb"], c * 128, n, par, lambda k: mid[:, slot, k, :n], [("mid", slot, k) for k in range(4)], nk=4)
                P.op("dve", lambda e, b3=b3: e.tensor_tensor(out=t1[:, :n], in0=sgp[:, :n], in1=ps[:, b3, :n], op=ALU.mult),
                     reads=[("ps", b3)] + sgp_k, writes=t1_k, free=[b3])
                b4 = proj(W["wab"], c * 128, n, par, lambda k: mid[:, slot, 4 + k, :n], [("mid", slot, 4 + k) for k in range(4)], nk=4)
                P.op("dve", lambda e, b4=b4: e.tensor_tensor(out=t2[:, :n], in0=sga[:, :n], in1=ps[:, b4, :n], op=ALU.mult),
                     reads=[("ps", b4)] + sga_k, writes=t2_k, free=[b4])
                P.op("dve", lambda e, c=c: e.tensor_tensor(out=mid[:, ms, c, :n], in0=t1[:, :n], in1=t2[:, :n], op=ALU.add),
                     reads=t1_k + t2_k, writes=[("mid", ms, c)])

        def p2b_tile(t0, n, ms, W, par):
            for f_ in range(8):
                b = proj(W["wo"], f_ * 128, n, par, lambda k: mid[:, ms, k, :n], [("mid", ms, k) for k in range(8)])
                P.op("dve", lambda e, b=b, f_=f_: e.tensor_tensor(out=hT[:, f_, t0:t0 + n], in0=hT[:, f_, t0:t0 + n], in1=ps[:, b, :n], op=ALU.add),
                     reads=[("ps", b)] + hk([f_], t0, n), writes=hk([f_], t0, n), free=[b])

        def p3_steps(slot, t0, n, gf):
            return norm_steps(t0, n, gf, (lambda k: mid[:, slot, k, :n]), (lambda k: [("mid", slot, k)]))

        def ffn_gate(j, n, W, par, sl):
            u2k = [("mid", sl, k) for k in range(8)]
            sb_i = j % 2
            b1 = proj(W["wg"], j * 128, n, par, lambda k: mid[:, sl, k, :n], u2k)
            P.op("act", lambda e: e.activation(out=sgt[sb_i][:, :n], in_=ps[:, b1, :n], func=AF.Silu),
                 reads=[("ps", b1)], writes=sgt_k[sb_i], free=[b1])

        def ffn_tile(ti, t0, n, nj, W, par, sl, have_g0=False, nxt=None):
            u2k = [("mid", sl, k) for k in range(8)]
            for j in range(nj):
                sb_i = j % 2
                if not (j == 0 and have_g0):
                    ffn_gate(j, n, W, par, sl)
                b2 = proj(W["wu"], j * 128, n, par, lambda k: mid[:, sl, k, :n], u2k)
                P.op("dve", lambda e, b2=b2, sb_i=sb_i, j=j: e.tensor_tensor(out=actb[:, j, :n], in0=sgt[sb_i][:, :n], in1=ps[:, b2, :n], op=ALU.mult),
                     reads=[("ps", b2)] + sgt_k[sb_i], writes=actb_k(j), free=[b2])
            if nxt is not None:
                ffn_gate(0, nxt[0], W, par, nxt[1])
            for f_ in range(8):
                b = nbank()
                def fd(e, b=b, f_=f_):
                    for j in range(nj):
                        ins = e.matmul(ps[:, b, :n], lhsT=W["wd"][:, j, f_ * 128:(f_ + 1) * 128], rhs=actb[:, j, :n], start=(j == 0), stop=(j == nj - 1))
                    return ins
                P.op("pe", fd, reads=[("w", par)] + [k_ for j in range(nj) for k_ in actb_k(j)], writes=[("ps", b)])
                P.op("dve", lambda e, b=b, f_=f_: e.tensor_tensor(out=hT[:, f_, t0:t0 + n], in0=hT[:, f_, t0:t0 + n], in1=ps[:, b, :n], op=ALU.add),
                     reads=[("ps", b)] + hk([f_], t0, n), writes=hk([f_], t0, n), free=[b])

        def final_steps(u, t0, n):
            b = SB
            steps = []
            def s_sq(k):
                def f():
                    P.op("act", lambda e: e.activation(out=sq[:, k % 2, :n], in_=hT[:, k, t0:t0 + n], func=AF.Square),
                         reads=hk([k], t0, n), writes=[("sq", k % 2)])
                    P.op("pe", lambda e: e.matmul(ps[:, b, :n], lhsT=ones[:, :], rhs=sq[:, k % 2, :n], start=(k == 0), stop=(k == 7)),
                         reads=[("sq", k % 2), ("ones",)], writes=[("ps", b)])
                return f
            def s_rs():
                P.op("act", lambda e: e.activation(out=ps[:, b, :n], in_=ps[:, b, :n], func=AF.Ln, bias=epsb[:], scale=1.0 / D),
                     reads=[("ps", b), ("eps",)], writes=[("ps", b)])
                P.op("act", lambda e: e.activation(out=ps[:, b, :n], in_=ps[:, b, :n], func=AF.Exp, scale=-0.5), reads=[("ps", b)], writes=[("ps", b)])
            def s_out(k):
                def f():
                    yb = k % 2
                    P.op("dve", lambda e: e.scalar_tensor_tensor(out=yout[yb][:, :n], in0=hT[:, k, t0:t0 + n], scalar=gfin[:, k:k + 1],
                                                                  in1=ps[:, b, :n], op0=ALU.mult, op1=ALU.mult),
                         reads=hk([k], t0, n) + [("ps", b), ("const",)], writes=yout_k[yb])
                    P.dma("sp", "outp%d" % yb, [lambda e: e.dma_start(out=out_d[u, k * 128:(k + 1) * 128, t0 - 256:t0 - 256 + n], in_=yout[yb][:, :n])],
                          reads=yout_k[yb])
                return f
            for k in range(8):
                steps.append(s_sq(k))
            steps.append(s_rs)
            for k in range(8):
                steps.append(s_out(k))
            return steps

        def issue_w(i):
            if i < _LIMIT:
                issue_weights(i)

        P.op("pool", lambda e: e.memset(ones[:], 1.0), writes=[("ones",)])
        P.op("pool", lambda e: e.memset(epsb[:], EPS), writes=[("eps",)])
        P.op("pool", lambda e: e.memset(qsc[0:64, 0:1], 0.125), writes=[("qsc",)])
        P.op("pool", lambda e: e.memset(qsc[64:128, 0:1], 0.0), writes=[("qsc",)])
        P.op("pool", lambda e: e.memset(qsc[0:64, 1:2], 0.0), writes=[("qsc",)])
        P.op("pool", lambda e: e.memset(qsc[64:128, 1:2], 0.125), writes=[("qsc",)])
        issue_w(0)
        pi = 0
        for u in range(_NU_RUN):
            unit_inputs(u)
            for l in range(DEPTH):
                if l == 0:
                    p0_tiles = [(0, 512), (512, 512), (1024, 512)]
                    main = [(128, 512), (640, 512), (1152, 256)]
                else:
                    p0_tiles = [(128, 512), (640, 512), (1152, 256)]
                    main = [(256, 512), (768, 512)]
                gm = gmix[:, l, :]
                gf = gffn[:, l, :]
                def zero_halo(side, a0):
                    for k in range(8):
                        P.op("dve", lambda e, k=k: e.tensor_scalar(out=hT[:, k, a0:a0 + 128], in0=hT[:, k, a0:a0 + 128],
                                                                   scalar1=hm[:, side:side + 1], scalar2=None, op0=ALU.mult),
                             reads=hk([k], a0, 128) + [("hm",)], writes=hk([k], a0, 128))
                k0 = ["A", "S", "A"][:len(p0_tiles)]
                k1 = ["S", "A", "S"][:len(main)]
                first2 = "Z" if k1[-1] == "A" else "A"
                k2 = [first2 if i % 2 == 0 else ("A" if first2 == "Z" else "Z") for i in range(len(main))]
                def nsteps(tile, kind):
                    t0_, n_ = tile
                    uf_, kf_, _ = ubuf(kind, n_)
                    return norm_steps(t0_, n_, gm, uf_, kf_)
                BGN[0] = 3
                ph = phases[pi]; issue_w(pi + 1); pi += 1
                if l == 0:
                    for f_ in nsteps(p0_tiles[0], k0[0]):
                        f_()
                else:
                    bg_flush()
                for ti, (t0, n) in enumerate(p0_tiles):
                    if ti + 1 < len(p0_tiles):
                        bg_add(nsteps(p0_tiles[ti + 1], k0[ti + 1]))
                    else:
                        bg_add(nsteps(main[0], k1[0]))
                    p0_tile(l, t0, n, ph["views"](ph["base"]), ph["par"], ubuf(k0[ti], n))
                    bg_flush()
                BGN[0] = 3
                ph = phases[pi]; issue_w(pi + 1); pi += 1
                for ti, (t0, n) in enumerate(main):
                    if ti + 1 < len(main):
                        bg_add(nsteps(main[ti + 1], k1[ti + 1]))
                    else:
                        bg_add(nsteps(main[0], k2[0]))
                    p1_tile(l, ti, t0, n, ph["views"](ph["base"]), ph["par"], ubuf(k1[ti], n),
                            z_hazard=(ti + 1 == len(main) and k2[0] == "Z"))
                    bg_flush()
                BGN[0] = 1
                ph = phases[pi]; issue_w(pi + 1); pi += 1
                mslot = [(ti + 3) % 4 for ti in range(len(main))]
                for ti, (t0, n) in enumerate(main):
                    if ti + 1 < len(main):
                        bg_add(nsteps(main[ti + 1], k2[ti + 1]))
                    p2a_tile(ti, t0, n, mslot[ti], ph["views"](ph["base"]), ph["par"], ubuf(k2[ti], n))
                    bg_flush()
                BGN[0] = 3
                ph = phases[pi]; issue_w(pi + 1); pi += 1
                for ti, (t0, n) in enumerate(main):
                    p2b_tile(t0, n, mslot[ti], ph["views"](ph["base"]), ph["par"])
                    bg_add(p3_steps(mslot[ti], t0, n, gf), tag=("p3", ti))
                BGN[0] = 2
                for fi, nj in enumerate(FCH):
                    ph = phases[pi]; issue_w(pi + 1); pi += 1
                    last = (fi == len(FCH) - 1)
                    for ti, (t0, n) in enumerate(main):
                        if fi == 0:
                            bg_require(("p3", ti))
                        hoist = fi > 0
                        nxt = (main[ti + 1][1], mslot[ti + 1]) if (hoist and ti + 1 < len(main)) else None
                        ffn_tile(ti, t0, n, nj, ph["views"](ph["base"]), ph["par"], mslot[ti], have_g0=(hoist and ti > 0), nxt=nxt)
                        if last and l == 0:
                            if ti == 0:
                                zero_halo(0, 128)
                                t0n, nn = 128, 512
                                uf_, kf_, _ = ubuf("A", nn)
                                bg_add(norm_steps(t0n, nn, gmix[:, 1, :], uf_, kf_))
                            if ti == len(main) - 1:
                                zero_halo(1, 1280)
                        if last and l == DEPTH - 1:
                            bg_add(final_steps(u, t0, n))
                if l < DEPTH - 1:
                    pass
            bg_flush()

        P.finalize()

        with nc.Block() as block:
            @block.tensor
            def _(e):
                P.replay("pe", e, sems)

            @block.scalar
            def _(e):
                P.replay("act", e, sems)

            @block.vector
            def _(e):
                P.replay("dve", e, sems)

            @block.gpsimd
            def _(e):
                P.replay("pool", e, sems)

            @block.sync
            def _(e):
                P.replay("sp", e, sems)
                e.wait_ge(sems["outp0"], P.totals["outp0"])
                e.wait_ge(sems["outp1"], P.totals["outp1"])
    return nc


def _consts():
    k = np.arange(128)[:, None]
    q = np.arange(128)[None, :]
    slopes = np.exp2(-8.0 * np.arange(1, 9, dtype=np.float64) / 8.0)
    biasT = np.zeros((128, 6, 4, 128), np.float32)
    for hh in range(2):
        for kb in range(3):
            dist = np.abs(q - (k + (kb - 1) * 128))
            for g in range(4):
                b = -slopes[4 * hh + g] * dist
                b = np.where(dist <= 128, b, NEG)
                biasT[:, hh * 3 + kb, g, :] = b
    return biasT.reshape(128, 3072), np.eye(128, dtype=np.float32)


def _unit_meta(s_unit):
    pos = 1024 * s_unit - HALO + np.arange(TT)
    valid = (pos >= 0) & (pos < SEQ)
    maskv = np.array([float(valid[128]), float(valid[1280])], np.float32)
    kb = np.where(valid, 0.0, NEG).astype(np.float32).reshape(12, 128).T.copy()
    corr = np.ones((2, 4, 8), np.float32)
    for g, w in enumerate(POOLW):
        for side, lo in ((0, 256), (1, 1272)):
            for i in range(8):
                p = pos[lo + i]
                if 0 <= p < SEQ:
                    lo_c = min(max(p - w // 2, 0), SEQ)
                    hi_c = min(max(p + w // 2, 0), SEQ)
                    corr[side, g, i] = float(w) / float(hi_c - lo_c)
    return maskv, kb, corr.reshape(64)


def _prep(x, norm_mix, w_in, w_pool_group, pool_scale, sink, w_pool_branch, w_attn_branch, w_out,
          norm_ffn, w_ffn_gate, w_ffn_up, w_ffn_down, norm_final, cores=range(NCORES)):
    x = np.asarray(x, np.float32)
    biasT, ident = _consts()
    f = lambda a: np.ascontiguousarray(np.asarray(a, np.float32))
    pl = lambda a: np.ascontiguousarray(np.asarray(a, np.float32).reshape(a.shape[0], -1, 128).transpose(2, 0, 1).reshape(128, -1))
    shared = dict(
        biasT=biasT, ident=ident, norm_mix=pl(norm_mix), w_in=f(w_in), w_pool_group=f(w_pool_group),
        pool_scale=pl(pool_scale), sink=f(sink).reshape(-1), w_pool_branch=f(w_pool_branch),
        w_attn_branch=f(w_attn_branch), w_out=f(w_out), norm_ffn=pl(norm_ffn), w_ffn_gate=f(w_ffn_gate),
        w_ffn_up=f(w_ffn_up), w_ffn_down=f(w_ffn_down), norm_final=pl(np.asarray(norm_final)[None, :]),
    )
    in_maps = []
    for core in cores:
        xT = np.zeros((NU, D, TT), np.float32)
        maskv = np.zeros((NU, 2), np.float32)
        kb = np.zeros((NU, 128, 12), np.float32)
        corr = np.zeros((NU, 64), np.float32)
        for uu in range(NU):
            gu = core * NU + uu
            bidx, su = gu // 4, gu % 4
            lo = 1024 * su - HALO
            a, b = max(lo, 0), min(lo + TT, SEQ)
            xT[uu, :, a - lo:b - lo] = x[bidx, a:b, :].T
            maskv[uu], kb[uu], corr[uu] = _unit_meta(su)
        m = dict(shared)
        m.update(xT=xT, hmask=maskv, kbias=kb, corr=corr)
        in_maps.append(m)
    return in_maps


def kernel(**inputs):
    in_maps = _prep(**inputs)
    nc = build_nc()
    res = run_bass_kernel_spmd(nc, in_maps, core_ids=list(range(NCORES)))
    out = np.empty((NB, SEQ, D), np.float32)
    for core in range(NCORES):
        o = res.results[core]["out"]
        for uu in range(NU):
            gu = core * NU + uu
            bidx, su = gu // 4, gu % 4
            out[bidx, 1024 * su:1024 * su + 1024, :] = o[uu].T
    return out
```

```python
import numpy as np
from contextlib import ExitStack
import concourse.bass as bass
import concourse.mybir as mybir
from concourse.bass_utils import run_bass_kernel_spmd

F32 = mybir.dt.float32
BF16 = mybir.dt.bfloat16
AF = mybir.ActivationFunctionType
ALU = mybir.AluOpType

D = 1024
SEQ = 4096
NB = 4
DEPTH = 2
DFF = 2816
INW = 3328
TT = 1536
OWN = 1024
HALO = 256
NU = 2
NCORES = 8
EPS = 1e-6
POOLW = (2, 4, 8, 16)
ARENA = 32768
FCH = (5, 5, 4, 4, 4)
NEG = -30000.0
_LIMIT = 999
_NU_RUN = NU
_P1 = 31


class _Op:
    __slots__ = ("eng", "fn", "fns", "sem", "deps", "needed", "ticket", "waits", "stream")


class Plan:
    def __init__(self):
        self.ops = []
        self.lists = {n: [] for n in ("pe", "act", "dve", "pool", "sp")}
        self.last_w = {}
        self.readers = {}

    def _add(self, op, reads, writes):
        oid = len(self.ops)
        deps = {}
        def dep(d):
            s = self.ops[d].stream
            if deps.get(s, -1) < d:
                deps[s] = d
        for k in reads:
            d = self.last_w.get(k)
            if d is not None:
                dep(d)
        for k in writes:
            d = self.last_w.get(k)
            if d is not None:
                dep(d)
            for d in self.readers.get(k, {}).values():
                dep(d)
        for k in writes:
            self.last_w[k] = oid
            self.readers[k] = {}
        for k in reads:
            self.readers.setdefault(k, {})[op.stream] = oid
        if op.eng == "pe":
            deps.pop("pe", None)
        op.deps = list(deps.values())
        op.needed = False
        self.ops.append(op)
        self.lists[op.eng].append(op)
        return oid

    def op(self, eng, fn, reads=(), writes=(), free=()):
        o = _Op()
        o.eng = eng; o.fn = fn; o.fns = None; o.sem = None; o.stream = eng
        r = self._add(o, reads, writes)
        for b in free:
            self.free_hook(b)
        return r

    def dma(self, eng, sem, fns, reads=(), writes=()):
        o = _Op()
        o.eng = eng; o.fn = None; o.fns = list(fns); o.sem = sem; o.stream = sem
        return self._add(o, reads, writes)

    def finalize(self):
        for o in self.ops:
            for d in o.deps:
                self.ops[d].needed = True
        cnt = {}
        for o in self.ops:
            if o.fns is not None:
                cnt[o.sem] = cnt.get(o.sem, 0) + 16 * len(o.fns)
                o.ticket = cnt[o.sem]
            elif o.needed:
                cnt[o.eng] = cnt.get(o.eng, 0) + 1
                o.ticket = cnt[o.eng]
        for o in self.ops:
            o.waits = [(self.ops[d].stream, self.ops[d].ticket) for d in o.deps]
        self.totals = cnt

    def replay(self, name, e, sems):
        waited = {}
        for o in self.lists[name]:
            for (s, v) in o.waits:
                if waited.get(s, 0) < v:
                    e.wait_ge(sems[s], v)
                    waited[s] = v
            if o.fns is not None:
                for f in o.fns:
                    f(e).then_inc(sems[o.sem], 16)
            else:
                ins = o.fn(e)
                if o.needed:
                    ins.then_inc(sems[name], 1)


def _blocks(t0, n):
    return range(t0 // 128, (t0 + n - 1) // 128 + 1)


def build_nc():
    nc = bass.Bass("TRN2", target_bir_lowering=False)
    dt = nc.dram_tensor
    xT_d = dt("xT", [NU, D, TT], F32, kind="ExternalInput").ap()
    hm_d = dt("hmask", [NU, 2], F32, kind="ExternalInput").ap()
    kbias_d = dt("kbias", [NU, 128, 12], F32, kind="ExternalInput").ap()
    corr_d = dt("corr", [NU, 64], F32, kind="ExternalInput").ap()
    biasT_d = dt("biasT", [128, 3072], F32, kind="ExternalInput").ap()
    ident_d = dt("ident", [128, 128], F32, kind="ExternalInput").ap()
    norm_mix_d = dt("norm_mix", [128, DEPTH * 8], F32, kind="ExternalInput").ap()
    w_in_d = dt("w_in", [DEPTH, D, INW], F32, kind="ExternalInput").ap()
    wpg_d = dt("w_pool_group", [DEPTH, 4, 128, 128], F32, kind="ExternalInput").ap()
    pscale_d = dt("pool_scale", [128, DEPTH * 4], F32, kind="ExternalInput").ap()
    sink_d = dt("sink", [DEPTH * 8], F32, kind="ExternalInput").ap()
    wpb_d = dt("w_pool_branch", [DEPTH, 512, D], F32, kind="ExternalInput").ap()
    wab_d = dt("w_attn_branch", [DEPTH, 512, D], F32, kind="ExternalInput").ap()
    wout_d = dt("w_out", [DEPTH, D, D], F32, kind="ExternalInput").ap()
    norm_ffn_d = dt("norm_ffn", [128, DEPTH * 8], F32, kind="ExternalInput").ap()
    wg_d = dt("w_ffn_gate", [DEPTH, D, DFF], F32, kind="ExternalInput").ap()
    wu_d = dt("w_ffn_up", [DEPTH, D, DFF], F32, kind="ExternalInput").ap()
    wd_d = dt("w_ffn_down", [DEPTH, DFF, D], F32, kind="ExternalInput").ap()
    norm_final_d = dt("norm_final", [128, 8], F32, kind="ExternalInput").ap()
    out_d = dt("out", [NU, D, OWN], F32, kind="ExternalOutput").ap()

    P = Plan()
    with ExitStack() as es:
        sb = lambda name, shape, dty: es.enter_context(nc.sbuf_tensor(name, shape, dty))
        hT = sb("hT", [128, 8, TT], F32)
        mid = sb("mid", [128, 4, 8, 512], BF16)
        kT = sb("kT", [128, 2, TT], BF16)
        vv = sb("vv", [128, 12, 256], BF16)
        zT = sb("zT", [128, 4, TT], BF16)
        hm = sb("hm", [128, 2], F32)
        ub = sb("ub", [128, 8, 512], BF16)
        sq = sb("sq", [128, 2, 512], BF16)
        arena = sb("arena", [128, ARENA], BF16)
        wk = sb("wk", [128, 9792], BF16)
        biasT = sb("biasTs", [128, 6, 512], BF16)
        ident = sb("idents", [128, 128], BF16)
        ones = sb("ones", [128, 128], BF16)
        gmix = sb("gmix", [128, DEPTH, 8], F32)
        gffn = sb("gffn", [128, DEPTH, 8], F32)
        gfin = sb("gfin", [128, 8], F32)
        pscale = sb("pscale", [128, DEPTH, 4], F32)
        esk = sb("esk", [128, 16], F32)
        esrow = [sb("esrow0", [128, 512], BF16), sb("esrow1", [128, 512], BF16)]
        kbias = sb("kbiass", [128, 12], F32)
        corr = sb("corrs", [128, 64], F32)
        epsb = sb("epsb", [128, 1], F32)
        qsc = sb("qsc", [128, 2], F32)
        rden_t = sb("rden", [128, 512], BF16)
        rden = rden_t[:, :]; rden_k = [("rden",)]
        ps = es.enter_context(nc.psum_tensor("ps", [128, 8, 512], F32))
        sem_names = ["pe", "act", "dve", "pool", "w0", "w1", "inp0", "inp1", "inp2", "inq0", "inq1", "inq2", "inc", "inm", "cb", "outp0", "outp1"]
        sems = {n: es.enter_context(nc.semaphore("s_" + n)) for n in sem_names}

        def wkv(start, length, dty=BF16):
            v = wk[:, start:start + length]
            return v.bitcast(dty) if dty != BF16 else v
        def wkk(start, length):
            return [("wk", b) for b in range(start // 32, (start + length - 1) // 32 + 1)]
        qe = wkv(0, 2048).rearrange("p (c t) -> p c t", c=4)
        qo = wkv(2048, 2048).rearrange("p (c t) -> p c t", c=4)
        qb_k = wkk(0, 4096)
        dg = wkv(4096, 2048).rearrange("p (c t) -> p c t", c=4)
        def dg_k(g): return wkk(4096 + 512 * g, 512)
        PT = wkv(8256, 1536).rearrange("p (c t) -> p c t", c=3)
        def PT_k(kb): return wkk(8256 + 512 * kb, 512)
        pa = wkv(6144, 1056, F32); pa_k = wkk(6144, 1056)
        pb = wkv(7200, 1056, F32); pb_k = wkk(7200, 1056)
        sgp = wkv(0, 1024, F32); sgp_k = wkk(0, 1024)
        sga = wkv(1024, 1024, F32); sga_k = wkk(1024, 1024)
        t1 = wkv(2048, 1024, F32); t1_k = wkk(2048, 1024)
        t2 = wkv(3072, 1024, F32); t2_k = wkk(3072, 1024)
        sgt = [wkv(0, 1024, F32), wkv(1024, 1024, F32)]
        sgt_k = [wkk(0, 1024), wkk(1024, 1024)]
        actb = wkv(2048, 2560).rearrange("p (c t) -> p c t", c=5)
        def actb_k(j): return wkk(2048 + 512 * j, 512)
        yout = [wkv(4608, 1024, F32), wkv(5632, 1024, F32)]
        yout_k = [wkk(4608, 1024), wkk(5632, 1024)]

        bank_ctr = [0]
        free_banks = list(range(7))
        def nbank():
            assert free_banks, "out of PSUM banks"
            return free_banks.pop(0)
        def bfree(b):
            assert b not in free_banks and 0 <= b < 7
            free_banks.append(b)
        P.free_hook = bfree
        SB = 7

        bg = []
        def bg_add(steps, tag=None):
            bg.extend((tag, f) for f in steps)
        def bg_step(k=1):
            for _ in range(k):
                if bg:
                    bg.pop(0)[1]()
        def bg_flush():
            while bg:
                bg.pop(0)[1]()
        bgp = []
        def bgp_step(k=1):
            for _ in range(k):
                if bgp:
                    bgp.pop(0)()
        def bg_require(tag):
            while any(t == tag for t, _ in bg):
                bg.pop(0)[1]()

        def hk(cs, t0, n):
            return [("h", c, b) for c in cs for b in _blocks(t0, n)]

        def load_consts():
            P.dma("sp", "inc", [
                lambda e: e.dma_start(out=gmix[:], in_=norm_mix_d.rearrange("p (l c) -> p l c", l=DEPTH)),
                lambda e: e.dma_start(out=gffn[:], in_=norm_ffn_d.rearrange("p (l c) -> p l c", l=DEPTH)),
                lambda e: e.dma_start(out=gfin[:], in_=norm_final_d),
                lambda e: e.dma_start(out=pscale[:], in_=pscale_d.rearrange("p (l c) -> p l c", l=DEPTH)),
                lambda e: e.dma_start(out=esk[:], in_=sink_d.partition_broadcast(128)),
            ], writes=[("const",), ("esk",)])
            P.dma("pool", "cb", [
                lambda e: e.dma_start(out=biasT[:], in_=biasT_d.rearrange("p (c t) -> p c t", c=6)),
                lambda e: e.dma_start(out=ident[:], in_=ident_d),
            ], writes=[("cb",)])
            P.op("act", lambda e: e.activation(out=esk[:], in_=esk[:], func=AF.Exp), reads=[("esk",)], writes=[("esk",)])
            for l_ in range(DEPTH):
                for hh_ in range(2):
                    for g_ in range(4):
                        p0_ = 32 * hh_
                        ix_ = l_ * 8 + 4 * hh_ + g_
                        P.op("dve", lambda e, l_=l_, p0_=p0_, g_=g_, ix_=ix_: e.tensor_copy(
                            out=esrow[l_][p0_:p0_ + 1, g_ * 128:(g_ + 1) * 128], in_=esk[p0_:p0_ + 1, ix_:ix_ + 1].to_broadcast([1, 128])),
                            reads=[("esk",)], writes=[("esrow",)])


        phase_ctr = [0]

        def wview(off, shape3):
            n = shape3[0] * shape3[1]
            return arena[:, off:off + n].rearrange("p (c f) -> p c f", c=shape3[0])

        def colsrc(w2d, a, b):
            return w2d[:, a:b].rearrange("(c p) f -> p c f", p=128)

        def rowsrc(w2d, a, b):
            return w2d[a:b, :].rearrange("(c p) f -> p c f", p=128)

        def mk_phase(kind, l, jlo=0, nj=0):
            if kind == "P0":
                size = 6 * 1024
                def views(base):
                    return dict(kd=wview(base, (8, 128)), vd=wview(base + 1024, (8, 128)), wz=wview(base + 2048, (8, 512)))
                def fns(base):
                    v = views(base)
                    return [lambda e: e.dma_start(out=v["kd"], in_=colsrc(w_in_d[l], 1024, 1152)),
                            lambda e: e.dma_start(out=v["vd"], in_=colsrc(w_in_d[l], 1152, 1280)),
                            lambda e: e.dma_start(out=v["wz"], in_=colsrc(w_in_d[l], 0, 512))]
            elif kind == "P1":
                size = 4096 + 512
                def views(base):
                    return dict(wq=wview(base, (8, 512)), wpg=wview(base + 4096, (4, 128)))
                def fns(base):
                    v = views(base)
                    wqv = v["wq"].rearrange("p k (g h d) -> p k g h d", g=4, h=2)
                    out = []
                    for h_ in range(2):
                        for g_ in range(4):
                            c0 = 512 + 256 * h_ + 64 * g_
                            out.append(lambda e, h_=h_, g_=g_, c0=c0: e.dma_start(out=wqv[:, :, g_, h_, :], in_=colsrc(w_in_d[l], c0, c0 + 64)))
                    out.append(lambda e: e.dma_start(out=v["wpg"], in_=wpg_d[l].rearrange("g p f -> p g f")))
                    return out
            elif kind == "P2a":
                size = 24576
                def views(base):
                    return dict(gp=wview(base, (8, 1024)), ga=wview(base + 8192, (8, 1024)),
                                wpb=wview(base + 16384, (4, 1024)), wab=wview(base + 20480, (4, 1024)))
                def fns(base):
                    v = views(base)
                    return [lambda e: e.dma_start(out=v["gp"], in_=colsrc(w_in_d[l], 1280, 2304)),
                            lambda e: e.dma_start(out=v["wpb"], in_=rowsrc(wpb_d[l], 0, 512)),
                            lambda e: e.dma_start(out=v["ga"], in_=colsrc(w_in_d[l], 2304, 3328)),
                            lambda e: e.dma_start(out=v["wab"][0:64, :, :], in_=wab_d[l][0:256, :].rearrange("(g d) f -> d g f", d=64)),
                            lambda e: e.dma_start(out=v["wab"][64:128, :, :], in_=wab_d[l][256:512, :].rearrange("(g d) f -> d g f", d=64))]
            elif kind == "P2b":
                size = 8192
                def views(base):
                    return dict(wo=wview(base, (8, 1024)))
                def fns(base):
                    v = views(base)
                    return [lambda e: e.dma_start(out=v["wo"], in_=colsrc(wout_d[l], 0, 1024))]
            elif kind == "F":
                size = nj * 3072
                def views(base):
                    return dict(wg=wview(base, (8, 128 * nj)), wu=wview(base + 1024 * nj, (8, 128 * nj)),
                                wd=wview(base + 2048 * nj, (nj, 1024)))
                def fns(base):
                    v = views(base)
                    a, b = 128 * jlo, 128 * (jlo + nj)
                    return [lambda e: e.dma_start(out=v["wg"], in_=colsrc(wg_d[l], a, b)),
                            lambda e: e.dma_start(out=v["wu"], in_=colsrc(wu_d[l], a, b)),
                            lambda e: e.dma_start(out=v["wd"], in_=rowsrc(wd_d[l], a, b))]
            return dict(kind=kind, l=l, size=size, fns=fns, views=views, jlo=jlo, nj=nj)

        phases = []
        for u in range(NU):
            for l in range(DEPTH):
                phases += [mk_phase("P0", l), mk_phase("P1", l), mk_phase("P2a", l), mk_phase("P2b", l)]
                j = 0
                for nj in FCH:
                    phases.append(mk_phase("F", l, j, nj))
                    j += nj
        for i, ph in enumerate(phases):
            ph["par"] = i % 2
            ph["base"] = 0 if i % 2 == 0 else ARENA - ph["size"]
            ph["u"] = i // (len(phases) // NU)
        for i in range(len(phases) - 1):
            assert phases[i]["size"] + phases[i + 1]["size"] <= ARENA

        def issue_weights(i):
            if i >= len(phases):
                return
            ph = phases[i]
            P.dma("pool", "w%d" % ph["par"], ph["fns"](ph["base"]), writes=[("w", ph["par"])])

        def norm_steps(t0, n, gain, dst, dst_keys):
            b = SB
            steps = []
            def s_sq(k):
                def f():
                    P.op("act", lambda e: e.activation(out=sq[:, k % 2, :n], in_=hT[:, k, t0:t0 + n], func=AF.Square),
                         reads=hk([k], t0, n), writes=[("sq", k % 2)])
                    P.op("pe", lambda e: e.matmul(ps[:, b, :n], lhsT=ones[:, :], rhs=sq[:, k % 2, :n], start=(k == 0), stop=(k == 7)),
                         reads=[("sq", k % 2), ("ones",)], writes=[("ps", b)])
                return f
            def s_rs():
                P.op("act", lambda e: e.activation(out=ps[:, b, :n], in_=ps[:, b, :n], func=AF.Ln, bias=epsb[:], scale=1.0 / D),
                     reads=[("ps", b), ("eps",)], writes=[("ps", b)])
                P.op("act", lambda e: e.activation(out=ps[:, b, :n], in_=ps[:, b, :n], func=AF.Exp, scale=-0.5), reads=[("ps", b)], writes=[("ps", b)])
            def s_sc(k):
                def f():
                    P.op("dve", lambda e: e.scalar_tensor_tensor(out=dst(k), in0=hT[:, k, t0:t0 + n], scalar=gain[:, k:k + 1],
                                                                  in1=ps[:, b, :n], op0=ALU.mult, op1=ALU.mult),
                         reads=hk([k], t0, n) + [("ps", b), ("const",)], writes=dst_keys(k))
                return f
            for k in range(8):
                steps.append(s_sq(k))
            steps.append(s_rs)
            for k in range(8):
                steps.append(s_sc(k))
            return steps

        def norm(t0, n, gain, dst, dst_keys):
            for f in norm_steps(t0, n, gain, dst, dst_keys):
                f()

        zflat = zT[:].rearrange("p g t -> p (g t)")[:, 0:4096].rearrange("p (k t) -> p k t", k=8)
        Z_ALL = [("z", g, bb) for g in range(4) for bb in range(12)]
        def ubuf(kind, n):
            if kind == "A":
                return (lambda k: ub[:, k, :n]), (lambda k: [("u", k)]), [("u", k) for k in range(8)]
            if kind == "S":
                return (lambda k: mid[:, 3, k, :n]), (lambda k: [("mid", 3, k)]), [("mid", 3, k) for k in range(8)]
            return (lambda k: zflat[:, k, :n]), (lambda k: Z_ALL), Z_ALL

        BGN = [1]
        def proj(wv, col0, n, par, rhs_fn, rhs_keys, nk=8):
            b = nbank()
            def f(e):
                for k in range(nk):
                    ins = e.matmul(ps[:, b, :n], lhsT=wv[:, k, col0:col0 + 128], rhs=rhs_fn(k), start=(k == 0), stop=(k == nk - 1))
                return ins
            P.op("pe", f, reads=[("w", par)] + rhs_keys, writes=[("ps", b)])
            bg_step(BGN[0])
            return b

        def unit_inputs(u):
            for part in range(3):
                a0 = 512 * part
                P.dma("sp" if u == 0 else "pool", ("inp%d" if u == 0 else "inq%d") % part, [(lambda e, c=c, a0=a0: e.dma_start(out=hT[:, c, a0:a0 + 512], in_=xT_d[u, c * 128:(c + 1) * 128, a0:a0 + 512]))
                                             for c in range(8)], writes=hk(range(8), a0, 512))
                if part == 0:
                    if u == 0:
                        load_consts()
                    small_inputs(u)

        def small_inputs(u):
            P.dma("sp", "inm", [
                lambda e: e.dma_start(out=kbias[:], in_=kbias_d[u]),
                lambda e: e.dma_start(out=corr[:], in_=corr_d[u].partition_broadcast(128)),
                lambda e: e.dma_start(out=hm[:], in_=hm_d[u].partition_broadcast(128)),
            ], writes=[("kbias",), ("corr",), ("hm",)])

        def p0_tile(l, t0, n, W, par, U):
            uf, _, UK = U
            b = proj(W["kd"], 0, n, par, uf, UK)
            P.op("act", lambda e, b=b: e.activation(out=kT[:, 0, t0:t0 + n], in_=ps[:, b, :n], func=AF.Copy),
                 reads=[("ps", b)], writes=[("k", 0, bb) for bb in _blocks(t0, n)], free=[b])
            for g in range(4):
                b = proj(W["wz"], g * 128, n, par, uf, UK)
                P.op("dve", lambda e, b=b, g=g: e.tensor_copy(out=zT[:, g, t0:t0 + n], in_=ps[:, b, :n]),
                     reads=[("ps", b)], writes=[("z", g, bb) for bb in _blocks(t0, n)], free=[b])
            for i in range(n // 128):
                b = nbank()
                gb = t0 // 128 + i
                def f(e, b=b, i=i):
                    for k in range(8):
                        ins = e.matmul(ps[:, b, 0:128], lhsT=uf(k)[:, i * 128:(i + 1) * 128], rhs=W["vd"][:, k, :], start=(k == 0), stop=(k == 7))
                    return ins
                P.op("pe", f, reads=[("w", par)] + UK, writes=[("ps", b)])
                P.op("act", lambda e, b=b, gb=gb: e.activation(out=vv[:, gb, 0:128], in_=ps[:, b, 0:128], func=AF.Copy),
                     reads=[("ps", b)], writes=[("v", gb)], free=[b])
                bg_step(2)

        def pool_steps(l, slot, t0, n, W, par):
            groups = []
            tails = []
            for g, w in enumerate(POOLW):
                steps = []
                base_t = t0 - w // 2
                L = n + w - 1
                zk = [("z", g, bb) for bb in _blocks(base_t, L)]
                cur = None; curk = None
                bufs = [(pa, pa_k), (pb, pb_k)]
                s_ = 1
                bi = 0
                while s_ < w:
                    Ln = L - s_
                    ob, obk = bufs[bi]
                    if cur is None:
                        steps.append(lambda ob=ob, obk=obk, Ln=Ln, s_=s_, base_t=base_t, g=g, zk=zk: P.op("pool", lambda e: e.tensor_tensor(
                            out=ob[:, :Ln], in0=zT[:, g, base_t:base_t + Ln], in1=zT[:, g, base_t + s_:base_t + s_ + Ln], op=ALU.add),
                            reads=zk, writes=obk))
                    else:
                        steps.append(lambda ob=ob, obk=obk, cur=cur, curk=curk, Ln=Ln, s_=s_: P.op("pool", lambda e: e.tensor_tensor(
                            out=ob[:, :Ln], in0=cur[:, :Ln], in1=cur[:, s_:s_ + Ln], op=ALU.add),
                            reads=curk, writes=obk))
                    cur, curk = ob, obk
                    L = Ln
                    s_ *= 2
                    bi ^= 1
                for side, lo in ((0, 256), (1, 1272)):
                    a0 = max(lo, t0); a1 = min(lo + 8, t0 + n)
                    if a1 > a0:
                        cc = side * 32 + g * 8 + (a0 - lo)
                        steps.append(lambda cur=cur, curk=curk, a0=a0, a1=a1, cc=cc: P.op("pool", lambda e: e.tensor_tensor(
                            out=cur[:, a0 - t0:a1 - t0], in0=cur[:, a0 - t0:a1 - t0], in1=corr[:, cc:cc + (a1 - a0)], op=ALU.mult),
                            reads=curk + [("corr",)], writes=curk))
                def fin_a(cur=cur, curk=curk, g=g, w=w):
                    P.op("dve", lambda e: e.scalar_tensor_tensor(
                        out=dg[:, g, :n], in0=cur[:, :n], scalar=1.0 / w, in1=zT[:, g, t0:t0 + n], op0=ALU.mult, op1=ALU.subtract),
                        reads=curk + [("z", g, bb) for bb in _blocks(t0, n)], writes=dg_k(g))
                steps.append(fin_a)
                def fin_b(g=g):
                    b = nbank()
                    P.op("pe", lambda e: e.matmul(ps[:, b, :n], lhsT=W["wpg"][:, g, :], rhs=dg[:, g, :n], start=True, stop=True),
                         reads=[("w", par)] + dg_k(g), writes=[("ps", b)])
                    P.op("dve", lambda e: e.tensor_scalar(out=mid[:, slot, g, :n], in0=ps[:, b, :n], scalar1=pscale[:, l, g:g + 1],
                                                           scalar2=None, op0=ALU.mult),
                         reads=[("ps", b), ("const",)], writes=[("mid", slot, g)], free=[b])
                groups.append(lambda steps=steps: [f() for f in steps])
                tails.append(fin_b)
            return groups + tails

        def q_chunk(c, n, W, par, U):
            uf, _, UK = U
            b = proj(W["wq"], c * 128, n, par, uf, UK)
            P.op("dve", lambda e: e.tensor_scalar(out=qe[:, c, :n], in0=ps[:, b, :n], scalar1=qsc[:, 0:1], scalar2=None, op0=ALU.mult),
                 reads=[("ps", b), ("qsc",)], writes=wkk(512 * c, 512))
            P.op("dve", lambda e: e.tensor_scalar(out=qo[:, c, :n], in0=ps[:, b, :n], scalar1=qsc[:, 1:2], scalar2=None, op0=ALU.mult),
                 reads=[("ps", b), ("qsc",)], writes=wkk(2048 + 512 * c, 512), free=[b])

        def p1_tile(l, ti, t0, n, W, par, U, z_hazard=False, pre_q=False, nxt=None):
            uf, _, UK = U
            slot = ti
            if z_hazard:
                BGN[0] = 2
            if not pre_q:
                for c in range(4):
                    q_chunk(c, n, W, par, U)
            bgp.extend(pool_steps(l, slot, t0, n, W, par))
            if z_hazard:
                bgp_step(16)
                BGN[0] = 3
            items = [(i, hh) for i in range(n // 128) for hh in range(2)]
            def stage_a(it):
                i, hh = it
                a = t0 + 128 * i
                bss = []
                for kb in range(3):
                    gb = a // 128 - 1 + kb
                    bs = nbank()
                    bss.append(bs)
                    def fst(e, bs=bs, kb=kb, gb=gb):
                        e.matmul(ps[:, bs, :], lhsT=ident[:, :], rhs=biasT[:, hh * 3 + kb, :], start=True, stop=False, skip_group_check=True)
                        src = qe if hh == 0 else qo
                        for g in range(4):
                            ins = e.matmul(ps[:, bs, g * 128:(g + 1) * 128], lhsT=kT[:, 0, gb * 128:(gb + 1) * 128],
                                           rhs=src[:, g, i * 128:(i + 1) * 128], start=False, stop=True, skip_group_check=True)
                        return ins
                    P.op("pe", fst, reads=[("cb",), ("k", 0, gb)] + qb_k, writes=[("ps", bs)])
                return bss
            def stage_b(it, bss):
                i, hh = it
                a = t0 + 128 * i
                bo = nbank(); bd = nbank()
                for kb in range(3):
                    gb = a // 128 - 1 + kb
                    bs = bss[kb]
                    P.op("act", lambda e, bs=bs, kb=kb, gb=gb: e.activation(out=PT[:, kb, :], in_=ps[:, bs, :], func=AF.Exp,
                                                                             bias=kbias[:, gb:gb + 1], scale=1.0),
                         reads=[("ps", bs), ("kbias",)], writes=PT_k(kb), free=[bs])
                    def fpv(e, kb=kb, gb=gb):
                        e.matmul(ps[:, bo, :], lhsT=vv[:, gb, 0:128], rhs=PT[:, kb, :], start=(kb == 0), stop=(kb == 2))
                        ins = e.matmul(ps[:, bd, :], lhsT=ones[:, :], rhs=PT[:, kb, :], start=(kb == 0), stop=False)
                        if kb == 2:
                            p0 = 32 * hh
                            ins = e.matmul(ps[:, bd, :], lhsT=ones[p0:p0 + 1, :], rhs=esrow[l][p0:p0 + 1, :], start=False, stop=True)
                        return ins
                    P.op("pe", fpv, reads=[("v", gb), ("ones",), ("esrow",)] + PT_k(kb), writes=[("ps", bo), ("ps", bd)])
                return bo, bd
            def stage_c(it, bo, bd):
                i, hh = it
                P.op("act", lambda e: e.activation(out=ps[:, bd, :], in_=ps[:, bd, :], func=AF.Ln), reads=[("ps", bd)], writes=[("ps", bd)])
                P.op("act", lambda e: e.activation(out=rden, in_=ps[:, bd, :], func=AF.Exp, scale=-1.0), reads=[("ps", bd)], writes=rden_k, free=[bd])
                psv = ps[:, bo, :].rearrange("p (g q) -> p g q", g=4)
                rdv = rden.rearrange("p (g q) -> p g q", g=4)
                r0 = 64 * hh
                P.op("dve", lambda e: e.tensor_tensor(
                    out=mid[r0:r0 + 64, slot, 4:8, i * 128:(i + 1) * 128], in0=psv[r0:r0 + 64, :, :], in1=rdv[r0:r0 + 64, :, :], op=ALU.mult),
                    reads=[("ps", bo)] + rden_k, writes=[("mid", slot, 4 + g_) for g_ in range(4)], free=[bo])
            A = {0: stage_a(items[0])}
            Bk = {0: stage_b(items[0], A[0])}
            if len(items) > 1:
                A[1] = stage_a(items[1])
            for j in range(len(items)):
                if j + 1 < len(items):
                    Bk[j + 1] = stage_b(items[j + 1], A[j + 1])
                if j + 2 < len(items):
                    A[j + 2] = stage_a(items[j + 2])
                stage_c(items[j], *Bk[j])
                bgp_step((8 + len(items) - 1) // len(items))
                bg_step(5)
                if nxt is not None and j >= len(items) - 2:
                    bg_require(nxt[2])
                    jj = j - (len(items) - 2)
                    q_chunk(2 * jj, nxt[0], W, par, nxt[1])
                    q_chunk(2 * jj + 1, nxt[0], W, par, nxt[1])
            bgp_step(16)

        def p2a_tile(ti, t0, n, ms, W, par, U):
            uf, _, UK = U
            slot = ti
            for c in range(8):
                b1 = proj(W["gp"], c * 128, n, par, uf, UK)
                P.op("act", lambda e, b1=b1: e.activation(out=sgp[:, :n], in_=ps[:, b1, :n], func=AF.Sigmoid),
                     reads=[("ps", b1)], writes=sgp_k, free=[b1])
                b2 = proj(W["ga"], c * 128, n, par, uf, UK)
                P.op("act", lambda e, b2=b2: e.activation(out=sga[:, :n], in_=ps[:, b2, :n], func=AF.Sigmoid),
                     reads=[("ps", b2)], writes=sga_k, free=[b2])
                b3 = proj(W["wpb"], c * 128, n, par, lambda k: mid[:, slot, k, :n], [("mid", slot, k) for k in range(4)], nk=4)
                P.op("dve", lambda e, b3=b3: e.tensor_tensor(out=t1[:, :n], in0=sgp[:, :n], in1=ps[:, b3, :n], op=ALU.mult),
                     reads=[("ps", b3)] + sgp_k, writes=t1_k, free=[b3])
                b4 = proj(W["wab"], c * 128, n, par, lambda k: mid[:, slot, 4 + k, :n], [("mid", slot, 4 + k) for k in range(4)], nk=4)
                P.op("dve", lambda e, b4=b4: e.tensor_tensor(out=t2[:, :n], in0=sga[:, :n], in1=ps[:, b4, :n], op=ALU.mult),
                     reads=[("ps", b4)] + sga_k, writes=t2_k, free=[b4])
                P.op("dve", lambda e, c=c: e.tensor_tensor(out=mid[:, ms, c, :n], in0=t1[:, :n], in1=t2[:, :n], op=ALU.add),
                     reads=t1_k + t2_k, writes=[("mid", ms, c)])

        def p2b_tile(t0, n, ms, W, par):
            for f_ in range(8):
                b = proj(W["wo"], f_ * 128, n, par, lambda k: mid[:, ms, k, :n], [("mid", ms, k) for k in range(8)])
                P.op("dve", lambda e, b=b, f_=f_: e.tensor_tensor(out=hT[:, f_, t0:t0 + n], in0=hT[:, f_, t0:t0 + n], in1=ps[:, b, :n], op=ALU.add),
                     reads=[("ps", b)] + hk([f_], t0, n), writes=hk([f_], t0, n), free=[b])

        def p3_steps(slot, t0, n, gf):
            return norm_steps(t0, n, gf, (lambda k: mid[:, slot, k, :n]), (lambda k: [("mid", slot, k)]))

        def ffn_gate(j, n, W, par, sl):
            u2k = [("mid", sl, k) for k in range(8)]
            sb_i = j % 2
            b1 = proj(W["wg"], j * 128, n, par, lambda k: mid[:, sl, k, :n], u2k)
            P.op("act", lambda e: e.activation(out=sgt[sb_i][:, :n], in_=ps[:, b1, :n], func=AF.Silu),
                 reads=[("ps", b1)], writes=sgt_k[sb_i], free=[b1])

        def ffn_tile(ti, t0, n, nj, W, par, sl, have_g0=False, nxt=None):
            u2k = [("mid", sl, k) for k in range(8)]
            for j in range(nj):
                sb_i = j % 2
                if not (j == 0 and have_g0):
                    ffn_gate(j, n, W, par, sl)
                b2 = proj(W["wu"], j * 128, n, par, lambda k: mid[:, sl, k, :n], u2k)
                P.op("dve", lambda e, b2=b2, sb_i=sb_i, j=j: e.tensor_tensor(out=actb[:, j, :n], in0=sgt[sb_i][:, :n], in1=ps[:, b2, :n], op=ALU.mult),
                     reads=[("ps", b2)] + sgt_k[sb_i], writes=actb_k(j), free=[b2])
            if nxt is not None:
                ffn_gate(0, nxt[0], W, par, nxt[1])
            for f_ in range(8):
                b = nbank()
                def fd(e, b=b, f_=f_):
                    for j in range(nj):
                        ins = e.matmul(ps[:, b, :n], lhsT=W["wd"][:, j, f_ * 128:(f_ + 1) * 128], rhs=actb[:, j, :n], start=(j == 0), stop=(j == nj - 1))
                    return ins
                P.op("pe", fd, reads=[("w", par)] + [k_ for j in range(nj) for k_ in actb_k(j)], writes=[("ps", b)])
                P.op("dve", lambda e, b=b, f_=f_: e.tensor_tensor(out=hT[:, f_, t0:t0 + n], in0=hT[:, f_, t0:t0 + n], in1=ps[:, b, :n], op=ALU.add),
                     reads=[("ps", b)] + hk([f_], t0, n), writes=hk([f_], t0, n), free=[b])

        def final_steps(u, t0, n):
            b = SB
            steps = []
            def s_sq(k):
                def f():
                    P.op("act", lambda e: e.activation(out=sq[:, k % 2, :n], in_=hT[:, k, t0:t0 + n], func=AF.Square),
                         reads=hk([k], t0, n), writes=[("sq", k % 2)])
                    P.op("pe", lambda e: e.matmul(ps[:, b, :n], lhsT=ones[:, :], rhs=sq[:, k % 2, :n], start=(k == 0), stop=(k == 7)),
                         reads=[("sq", k % 2), ("ones",)], writes=[("ps", b)])
                return f
            def s_rs():
                P.op("act", lambda e: e.activation(out=ps[:, b, :n], in_=ps[:, b, :n], func=AF.Ln, bias=epsb[:], scale=1.0 / D),
                     reads=[("ps", b), ("eps",)], writes=[("ps", b)])
                P.op("act", lambda e: e.activation(out=ps[:, b, :n], in_=ps[:, b, :n], func=AF.Exp, scale=-0.5), reads=[("ps", b)], writes=[("ps", b)])
            def s_out(k):
                def f():
                    yb = k % 2
                    P.op("dve", lambda e: e.scalar_tensor_tensor(out=yout[yb][:, :n], in0=hT[:, k, t0:t0 + n], scalar=gfin[:, k:k + 1],
                                                                  in1=ps[:, b, :n], op0=ALU.mult, op1=ALU.mult),
                         reads=hk([k], t0, n) + [("ps", b), ("const",)], writes=yout_k[yb])
                    P.dma("sp", "outp%d" % yb, [lambda e: e.dma_start(out=out_d[u, k * 128:(k + 1) * 128, t0 - 256:t0 - 256 + n], in_=yout[yb][:, :n])],
                          reads=yout_k[yb])
                return f
            for k in range(8):
                steps.append(s_sq(k))
            steps.append(s_rs)
            for k in range(8):
                steps.append(s_out(k))
            return steps

        def issue_w(i):
            if i < _LIMIT:
                issue_weights(i)

        P.op("pool", lambda e: e.memset(ones[:], 1.0), writes=[("ones",)])
        P.op("pool", lambda e: e.memset(epsb[:], EPS), writes=[("eps",)])
        P.op("pool", lambda e: e.memset(qsc[0:64, 0:1], 0.125), writes=[("qsc",)])
        P.op("pool", lambda e: e.memset(qsc[64:128, 0:1], 0.0), writes=[("qsc",)])
        P.op("pool", lambda e: e.memset(qsc[0:64, 1:2], 0.0), writes=[("qsc",)])
        P.op("pool", lambda e: e.memset(qsc[64:128, 1:2], 0.125), writes=[("qsc",)])
        issue_w(0)
        pi = 0
        for u in range(_NU_RUN):
            unit_inputs(u)
            for l in range(DEPTH):
                if l == 0:
                    p0_tiles = [(0, 512), (512, 512), (1024, 512)]
                    main = [(128, 512), (640, 512), (1152, 256)]
                else:
                    p0_tiles = [(128, 512), (640, 512), (1152, 256)]
                    main = [(256, 512), (768, 512)]
                gm = gmix[:, l, :]
                gf = gffn[:, l, :]
                def zero_halo(side, a0):
                    for k in range(8):
                        P.op("dve", lambda e, k=k: e.tensor_scalar(out=hT[:, k, a0:a0 + 128], in0=hT[:, k, a0:a0 + 128],
                                                                   scalar1=hm[:, side:side + 1], scalar2=None, op0=ALU.mult),
                             reads=hk([k], a0, 128) + [("hm",)], writes=hk([k], a0, 128))
                k0 = ["A", "S", "A"][:len(p0_tiles)]
                k1 = ["S", "A", "S"][:len(main)]
                first2 = "Z" if k1[-1] == "A" else "A"
                k2 = [first2 if i % 2 == 0 else ("A" if first2 == "Z" else "Z") for i in range(len(main))]
                def nsteps(tile, kind):
                    t0_, n_ = tile
                    uf_, kf_, _ = ubuf(kind, n_)
                    return norm_steps(t0_, n_, gm, uf_, kf_)
                BGN[0] = 3
                ph = phases[pi]; issue_w(pi + 1); pi += 1
                if l == 0:
                    for f_ in nsteps(p0_tiles[0], k0[0]):
                        f_()
                else:
                    bg_flush()
                for ti, (t0, n) in enumerate(p0_tiles):
                    if ti + 1 < len(p0_tiles):
                        bg_add(nsteps(p0_tiles[ti + 1], k0[ti + 1]))
                    else:
                        bg_add(nsteps(main[0], k1[0]))
                    p0_tile(l, t0, n, ph["views"](ph["base"]), ph["par"], ubuf(k0[ti], n))
                    bg_flush()
                BGN[0] = 3
                ph = phases[pi]; issue_w(pi + 1); pi += 1
                for ti, (t0, n) in enumerate(main):
                    nxt = None
                    if ti + 1 < len(main):
                        bg_add(nsteps(main[ti + 1], k1[ti + 1]), tag=("n1", ti + 1))
                        nxt = (main[ti + 1][1], ubuf(k1[ti + 1], main[ti + 1][1]), ("n1", ti + 1))
                    else:
                        bg_add(nsteps(main[0], k2[0]))
                    p1_tile(l, ti, t0, n, ph["views"](ph["base"]), ph["par"], ubuf(k1[ti], n),
                            z_hazard=(ti + 1 == len(main) and k2[0] == "Z"), pre_q=(ti > 0), nxt=nxt)
                    bg_flush()
                BGN[0] = 1
                ph = phases[pi]; issue_w(pi + 1); pi += 1
                mslot = [(ti + 3) % 4 for ti in range(len(main))]
                for ti, (t0, n) in enumerate(main):
                    if ti + 1 < len(main):
                        bg_add(nsteps(main[ti + 1], k2[ti + 1]))
                    p2a_tile(ti, t0, n, mslot[ti], ph["views"](ph["base"]), ph["par"], ubuf(k2[ti], n))
                    bg_flush()
                BGN[0] = 3
                ph = phases[pi]; issue_w(pi + 1); pi += 1
                for ti, (t0, n) in enumerate(main):
                    p2b_tile(t0, n, mslot[ti], ph["views"](ph["base"]), ph["par"])
                    bg_add(p3_steps(mslot[ti], t0, n, gf), tag=("p3", ti))
                BGN[0] = 2
                for fi, nj in enumerate(FCH):
                    ph = phases[pi]; issue_w(pi + 1); pi += 1
                    last = (fi == len(FCH) - 1)
                    for ti, (t0, n) in enumerate(main):
                        if fi == 0:
                            bg_require(("p3", ti))
                        hoist = fi > 0
                        nxt = (main[ti + 1][1], mslot[ti + 1]) if (hoist and ti + 1 < len(main)) else None
                        ffn_tile(ti, t0, n, nj, ph["views"](ph["base"]), ph["par"], mslot[ti], have_g0=(hoist and ti > 0), nxt=nxt)
                        if last and l == 0:
                            if ti == 0:
                                zero_halo(0, 128)
                                t0n, nn = 128, 512
                                uf_, kf_, _ = ubuf("A", nn)
                                bg_add(norm_steps(t0n, nn, gmix[:, 1, :], uf_, kf_))
                            if ti == len(main) - 1:
                                zero_halo(1, 1280)
                        if last and l == DEPTH - 1:
                            bg_add(final_steps(u, t0, n))
                if l < DEPTH - 1:
                    pass
            bg_flush()

        P.finalize()

        with nc.Block() as block:
            @block.tensor
            def _(e):
                P.replay("pe", e, sems)

            @block.scalar
            def _(e):
                P.replay("act", e, sems)

            @block.vector
            def _(e):
                P.replay("dve", e, sems)

            @block.gpsimd
            def _(e):
                P.replay("pool", e, sems)

            @block.sync
            def _(e):
                P.replay("sp", e, sems)
                e.wait_ge(sems["outp0"], P.totals["outp0"])
                e.wait_ge(sems["outp1"], P.totals["outp1"])
    return nc


def _consts():
    k = np.arange(128)[:, None]
    q = np.arange(128)[None, :]
    slopes = np.exp2(-8.0 * np.arange(1, 9, dtype=np.float64) / 8.0)
    biasT = np.zeros((128, 6, 4, 128), np.float32)
    for hh in range(2):
        for kb in range(3):
            dist = np.abs(q - (k + (kb - 1) * 128))
            for g in range(4):
                b = -slopes[4 * hh + g] * dist
                b = np.where(dist <= 128, b, NEG)
                biasT[:, hh * 3 + kb, g, :] = b
    return biasT.reshape(128, 3072), np.eye(128, dtype=np.float32)


def _unit_meta(s_unit):
    pos = 1024 * s_unit - HALO + np.arange(TT)
    valid = (pos >= 0) & (pos < SEQ)
    maskv = np.array([float(valid[128]), float(valid[1280])], np.float32)
    kb = np.where(valid, 0.0, NEG).astype(np.float32).reshape(12, 128).T.copy()
    corr = np.ones((2, 4, 8), np.float32)
    for g, w in enumerate(POOLW):
        for side, lo in ((0, 256), (1, 1272)):
            for i in range(8):
                p = pos[lo + i]
                if 0 <= p < SEQ:
                    lo_c = min(max(p - w // 2, 0), SEQ)
                    hi_c = min(max(p + w // 2, 0), SEQ)
                    corr[side, g, i] = float(w) / float(hi_c - lo_c)
    return maskv, kb, corr.reshape(64)


def _prep(x, norm_mix, w_in, w_pool_group, pool_scale, sink, w_pool_branch, w_attn_branch, w_out,
          norm_ffn, w_ffn_gate, w_ffn_up, w_ffn_down, norm_final, cores=range(NCORES)):
    x = np.asarray(x, np.float32)
    biasT, ident = _consts()
    f = lambda a: np.ascontiguousarray(np.asarray(a, np.float32))
    pl = lambda a: np.ascontiguousarray(np.asarray(a, np.float32).reshape(a.shape[0], -1, 128).transpose(2, 0, 1).reshape(128, -1))
    shared = dict(
        biasT=biasT, ident=ident, norm_mix=pl(norm_mix), w_in=f(w_in), w_pool_group=f(w_pool_group),
        pool_scale=pl(pool_scale), sink=f(sink).reshape(-1), w_pool_branch=f(w_pool_branch),
        w_attn_branch=f(w_attn_branch), w_out=f(w_out), norm_ffn=pl(norm_ffn), w_ffn_gate=f(w_ffn_gate),
        w_ffn_up=f(w_ffn_up), w_ffn_down=f(w_ffn_down), norm_final=pl(np.asarray(norm_final)[None, :]),
    )
    in_maps = []
    for core in cores:
        xT = np.zeros((NU, D, TT), np.float32)
        maskv = np.zeros((NU, 2), np.float32)
        kb = np.zeros((NU, 128, 12), np.float32)
        corr = np.zeros((NU, 64), np.float32)
        for uu in range(NU):
            gu = core * NU + uu
            bidx, su = gu // 4, gu % 4
            lo = 1024 * su - HALO
            a, b = max(lo, 0), min(lo + TT, SEQ)
            xT[uu, :, a - lo:b - lo] = x[bidx, a:b, :].T
            maskv[uu], kb[uu], corr[uu] = _unit_meta(su)
        m = dict(shared)
        m.update(xT=xT, hmask=maskv, kbias=kb, corr=corr)
        in_maps.append(m)
    return in_maps


def kernel(**inputs):
    in_maps = _prep(**inputs)
    nc = build_nc()
    res = run_bass_kernel_spmd(nc, in_maps, core_ids=list(range(NCORES)))
    out = np.empty((NB, SEQ, D), np.float32)
    for core in range(NCORES):
        o = res.results[core]["out"]
        for uu in range(NU):
            gu = core * NU + uu
            bidx, su = gu // 4, gu % 4
            out[bidx, 1024 * su:1024 * su + 1024, :] = o[uu].T
    return out
```

```python
import numpy as np
from contextlib import ExitStack
import concourse.bass as bass
import concourse.mybir as mybir
from concourse.bass_utils import run_bass_kernel_spmd

F32 = mybir.dt.float32
BF16 = mybir.dt.bfloat16
AF = mybir.ActivationFunctionType
ALU = mybir.AluOpType

D = 1024
SEQ = 4096
NB = 4
DEPTH = 2
DFF = 2816
INW = 3328
TT = 1536
OWN = 1024
HALO = 256
NU = 2
NCORES = 8
EPS = 1e-6
POOLW = (2, 4, 8, 16)
ARENA = 33792
FCH = (6, 5, 6, 5)
NEG = -30000.0
_LIMIT = 999
_NU_RUN = NU
_P1 = 31


class _Op:
    __slots__ = ("eng", "fn", "fns", "sem", "deps", "needed", "ticket", "waits", "stream")


class Plan:
    def __init__(self):
        self.ops = []
        self.lists = {n: [] for n in ("pe", "act", "dve", "pool", "sp")}
        self.last_w = {}
        self.readers = {}

    def _add(self, op, reads, writes):
        oid = len(self.ops)
        deps = {}
        def dep(d):
            s = self.ops[d].stream
            if deps.get(s, -1) < d:
                deps[s] = d
        for k in reads:
            d = self.last_w.get(k)
            if d is not None:
                dep(d)
        for k in writes:
            d = self.last_w.get(k)
            if d is not None:
                dep(d)
            for d in self.readers.get(k, {}).values():
                dep(d)
        for k in writes:
            self.last_w[k] = oid
            self.readers[k] = {}
        for k in reads:
            self.readers.setdefault(k, {})[op.stream] = oid
        if op.eng == "pe":
            deps.pop("pe", None)
        op.deps = list(deps.values())
        op.needed = False
        self.ops.append(op)
        self.lists[op.eng].append(op)
        return oid

    def op(self, eng, fn, reads=(), writes=(), free=()):
        o = _Op()
        o.eng = eng; o.fn = fn; o.fns = None; o.sem = None; o.stream = eng
        r = self._add(o, reads, writes)
        for b in free:
            self.free_hook(b)
        return r

    def dma(self, eng, sem, fns, reads=(), writes=()):
        o = _Op()
        o.eng = eng; o.fn = None; o.fns = list(fns); o.sem = sem; o.stream = sem
        return self._add(o, reads, writes)

    def finalize(self):
        for o in self.ops:
            for d in o.deps:
                self.ops[d].needed = True
        cnt = {}
        for o in self.ops:
            if o.fns is not None:
                cnt[o.sem] = cnt.get(o.sem, 0) + 16 * len(o.fns)
                o.ticket = cnt[o.sem]
            elif o.needed:
                cnt[o.eng] = cnt.get(o.eng, 0) + 1
                o.ticket = cnt[o.eng]
        for o in self.ops:
            o.waits = [(self.ops[d].stream, self.ops[d].ticket) for d in o.deps]
        self.totals = cnt

    def replay(self, name, e, sems):
        waited = {}
        for o in self.lists[name]:
            for (s, v) in o.waits:
                if waited.get(s, 0) < v:
                    e.wait_ge(sems[s], v)
                    waited[s] = v
            if o.fns is not None:
                for f in o.fns:
                    f(e).then_inc(sems[o.sem], 16)
            else:
                ins = o.fn(e)
                if o.needed:
                    ins.then_inc(sems[name], 1)


def _blocks(t0, n):
    return range(t0 // 128, (t0 + n - 1) // 128 + 1)


def build_nc():
    nc = bass.Bass("TRN2", target_bir_lowering=False)
    dt = nc.dram_tensor
    xT_d = dt("xT", [NU, D, TT], F32, kind="ExternalInput").ap()
    hm_d = dt("hmask", [NU, 2], F32, kind="ExternalInput").ap()
    kbias_d = dt("kbias", [NU, 128, 12], F32, kind="ExternalInput").ap()
    corr_d = dt("corr", [NU, 64], F32, kind="ExternalInput").ap()
    biasT_d = dt("biasT", [128, 3072], F32, kind="ExternalInput").ap()
    ident_d = dt("ident", [128, 128], F32, kind="ExternalInput").ap()
    norm_mix_d = dt("norm_mix", [128, DEPTH * 8], F32, kind="ExternalInput").ap()
    w_in_d = dt("w_in", [DEPTH, D, INW], F32, kind="ExternalInput").ap()
    wpg_d = dt("w_pool_group", [DEPTH, 4, 128, 128], F32, kind="ExternalInput").ap()
    pscale_d = dt("pool_scale", [128, DEPTH * 4], F32, kind="ExternalInput").ap()
    sink_d = dt("sink", [DEPTH * 8], F32, kind="ExternalInput").ap()
    wpb_d = dt("w_pool_branch", [DEPTH, 512, D], F32, kind="ExternalInput").ap()
    wab_d = dt("w_attn_branch", [DEPTH, 512, D], F32, kind="ExternalInput").ap()
    wout_d = dt("w_out", [DEPTH, D, D], F32, kind="ExternalInput").ap()
    norm_ffn_d = dt("norm_ffn", [128, DEPTH * 8], F32, kind="ExternalInput").ap()
    wg_d = dt("w_ffn_gate", [DEPTH, D, DFF], F32, kind="ExternalInput").ap()
    wu_d = dt("w_ffn_up", [DEPTH, D, DFF], F32, kind="ExternalInput").ap()
    wd_d = dt("w_ffn_down", [DEPTH, DFF, D], F32, kind="ExternalInput").ap()
    norm_final_d = dt("norm_final", [128, 8], F32, kind="ExternalInput").ap()
    out_d = dt("out", [NU, D, OWN], F32, kind="ExternalOutput").ap()

    P = Plan()
    with ExitStack() as es:
        sb = lambda name, shape, dty: es.enter_context(nc.sbuf_tensor(name, shape, dty))
        hT = sb("hT", [128, 8, TT], F32)
        mid = sb("mid", [128, 4, 8, 512], BF16)
        kT = sb("kT", [128, 1, TT], BF16)
        vv = sb("vv", [128, 12, 128], BF16)
        zT = sb("zT", [128, 4, TT], BF16)
        hm = sb("hm", [128, 2], F32)
        ub = sb("ub", [128, 8, 512], BF16)
        sq = sb("sq", [128, 2, 512], BF16)
        arena = sb("arena", [128, ARENA], BF16)
        wk = sb("wk", [128, 9792], BF16)
        biasT = sb("biasTs", [128, 6, 512], BF16)
        ident = sb("idents", [128, 128], BF16)
        ones = sb("ones", [128, 128], BF16)
        gmix = sb("gmix", [128, DEPTH, 8], F32)
        gffn = sb("gffn", [128, DEPTH, 8], F32)
        gfin = sb("gfin", [128, 8], F32)
        pscale = sb("pscale", [128, DEPTH, 4], F32)
        esk = sb("esk", [128, 16], F32)
        esrow = [sb("esrow0", [128, 512], BF16), sb("esrow1", [128, 512], BF16)]
        kbias = sb("kbiass", [128, 12], F32)
        corr = sb("corrs", [128, 64], F32)
        epsb = sb("epsb", [128, 1], F32)
        qsc = sb("qsc", [128, 2], F32)
        rden_t = sb("rden", [128, 512], BF16)
        rden = rden_t[:, :]; rden_k = [("rden",)]
        ps = es.enter_context(nc.psum_tensor("ps", [128, 8, 512], F32))
        sem_names = ["pe", "act", "dve", "pool", "w0", "w1", "inp0", "inp1", "inp2", "inq0", "inq1", "inq2", "inc", "inm", "cb", "outp0", "outp1"]
        sems = {n: es.enter_context(nc.semaphore("s_" + n)) for n in sem_names}

        def wkv(start, length, dty=BF16):
            v = wk[:, start:start + length]
            return v.bitcast(dty) if dty != BF16 else v
        def wkk(start, length):
            return [("wk", b) for b in range(start // 32, (start + length - 1) // 32 + 1)]
        qe = wkv(0, 2048).rearrange("p (c t) -> p c t", c=4)
        qo = wkv(2048, 2048).rearrange("p (c t) -> p c t", c=4)
        qb_k = wkk(0, 4096)
        dg = wkv(4096, 2048).rearrange("p (c t) -> p c t", c=4)
        def dg_k(g): return wkk(4096 + 512 * g, 512)
        PT = wkv(8256, 1536).rearrange("p (c t) -> p c t", c=3)
        def PT_k(kb): return wkk(8256 + 512 * kb, 512)
        pa = wkv(6144, 1056, F32); pa_k = wkk(6144, 1056)
        pb = wkv(7200, 1056, F32); pb_k = wkk(7200, 1056)
        sgp = wkv(0, 1024, F32); sgp_k = wkk(0, 1024)
        sga = wkv(1024, 1024, F32); sga_k = wkk(1024, 1024)
        t1 = wkv(2048, 1024, F32); t1_k = wkk(2048, 1024)
        t2 = wkv(3072, 1024, F32); t2_k = wkk(3072, 1024)
        sgt = [wkv(0, 1024, F32), wkv(1024, 1024, F32)]
        sgt_k = [wkk(0, 1024), wkk(1024, 1024)]
        actb = wkv(2048, 3072).rearrange("p (c t) -> p c t", c=6)
        def actb_k(j): return wkk(2048 + 512 * j, 512)
        yout = [wkv(5632, 1024, F32), wkv(6656, 1024, F32)]
        yout_k = [wkk(5632, 1024), wkk(6656, 1024)]

        bank_ctr = [0]
        free_banks = list(range(7))
        def nbank():
            assert free_banks, "out of PSUM banks"
            return free_banks.pop(0)
        def bfree(b):
            assert b not in free_banks and 0 <= b < 7
            free_banks.append(b)
        P.free_hook = bfree
        SB = 7

        bg = []
        def bg_add(steps, tag=None):
            bg.extend((tag, f) for f in steps)
        def bg_step(k=1):
            for _ in range(k):
                if bg:
                    bg.pop(0)[1]()
        def bg_flush():
            while bg:
                bg.pop(0)[1]()
        bgp = []
        def bgp_step(k=1):
            for _ in range(k):
                if bgp:
                    bgp.pop(0)()
        def bg_require(tag):
            while any(t == tag for t, _ in bg):
                bg.pop(0)[1]()

        def hk(cs, t0, n):
            return [("h", c, b) for c in cs for b in _blocks(t0, n)]

        def load_consts():
            P.dma("sp", "inc", [
                lambda e: e.dma_start(out=gmix[:], in_=norm_mix_d.rearrange("p (l c) -> p l c", l=DEPTH)),
                lambda e: e.dma_start(out=gffn[:], in_=norm_ffn_d.rearrange("p (l c) -> p l c", l=DEPTH)),
                lambda e: e.dma_start(out=gfin[:], in_=norm_final_d),
                lambda e: e.dma_start(out=pscale[:], in_=pscale_d.rearrange("p (l c) -> p l c", l=DEPTH)),
                lambda e: e.dma_start(out=esk[:], in_=sink_d.partition_broadcast(128)),
            ], writes=[("const",), ("esk",)])
            P.dma("pool", "cb", [
                lambda e: e.dma_start(out=biasT[:], in_=biasT_d.rearrange("p (c t) -> p c t", c=6)),
                lambda e: e.dma_start(out=ident[:], in_=ident_d),
            ], writes=[("cb",)])
            P.op("act", lambda e: e.activation(out=esk[:], in_=esk[:], func=AF.Exp), reads=[("esk",)], writes=[("esk",)])
            for l_ in range(DEPTH):
                for hh_ in range(2):
                    for g_ in range(4):
                        p0_ = 32 * hh_
                        ix_ = l_ * 8 + 4 * hh_ + g_
                        P.op("dve", lambda e, l_=l_, p0_=p0_, g_=g_, ix_=ix_: e.tensor_copy(
                            out=esrow[l_][p0_:p0_ + 1, g_ * 128:(g_ + 1) * 128], in_=esk[p0_:p0_ + 1, ix_:ix_ + 1].to_broadcast([1, 128])),
                            reads=[("esk",)], writes=[("esrow",)])


        phase_ctr = [0]

        def wview(off, shape3):
            n = shape3[0] * shape3[1]
            return arena[:, off:off + n].rearrange("p (c f) -> p c f", c=shape3[0])

        def colsrc(w2d, a, b):
            return w2d[:, a:b].rearrange("(c p) f -> p c f", p=128)

        def rowsrc(w2d, a, b):
            return w2d[a:b, :].rearrange("(c p) f -> p c f", p=128)

        def mk_phase(kind, l, jlo=0, nj=0):
            if kind == "P0":
                size = 6 * 1024
                def views(base):
                    return dict(kd=wview(base, (8, 128)), vd=wview(base + 1024, (8, 128)), wz=wview(base + 2048, (8, 512)))
                def fns(base):
                    v = views(base)
                    return [lambda e: e.dma_start(out=v["kd"], in_=colsrc(w_in_d[l], 1024, 1152)),
                            lambda e: e.dma_start(out=v["vd"], in_=colsrc(w_in_d[l], 1152, 1280)),
                            lambda e: e.dma_start(out=v["wz"], in_=colsrc(w_in_d[l], 0, 512))]
            elif kind == "P1":
                size = 4096 + 512
                def views(base):
                    return dict(wq=wview(base, (8, 512)), wpg=wview(base + 4096, (4, 128)))
                def fns(base):
                    v = views(base)
                    wqv = v["wq"].rearrange("p k (g h d) -> p k g h d", g=4, h=2)
                    out = []
                    for h_ in range(2):
                        for g_ in range(4):
                            c0 = 512 + 256 * h_ + 64 * g_
                            out.append(lambda e, h_=h_, g_=g_, c0=c0: e.dma_start(out=wqv[:, :, g_, h_, :], in_=colsrc(w_in_d[l], c0, c0 + 64)))
                    out.append(lambda e: e.dma_start(out=v["wpg"], in_=wpg_d[l].rearrange("g p f -> p g f")))
                    return out
            elif kind == "P2a":
                size = 24576
                def views(base):
                    return dict(gp=wview(base, (8, 1024)), ga=wview(base + 8192, (8, 1024)),
                                wpb=wview(base + 16384, (4, 1024)), wab=wview(base + 20480, (4, 1024)))
                def fns(base):
                    v = views(base)
                    return [lambda e: e.dma_start(out=v["gp"], in_=colsrc(w_in_d[l], 1280, 2304)),
                            lambda e: e.dma_start(out=v["wpb"], in_=rowsrc(wpb_d[l], 0, 512)),
                            lambda e: e.dma_start(out=v["ga"], in_=colsrc(w_in_d[l], 2304, 3328)),
                            lambda e: e.dma_start(out=v["wab"][0:64, :, :], in_=wab_d[l][0:256, :].rearrange("(g d) f -> d g f", d=64)),
                            lambda e: e.dma_start(out=v["wab"][64:128, :, :], in_=wab_d[l][256:512, :].rearrange("(g d) f -> d g f", d=64))]
            elif kind == "P2b":
                size = 8192
                def views(base):
                    return dict(wo=wview(base, (8, 1024)))
                def fns(base):
                    v = views(base)
                    return [lambda e: e.dma_start(out=v["wo"], in_=colsrc(wout_d[l], 0, 1024))]
            elif kind == "F":
                size = nj * 3072
                def views(base):
                    return dict(wg=wview(base, (8, 128 * nj)), wu=wview(base + 1024 * nj, (8, 128 * nj)),
                                wd=wview(base + 2048 * nj, (nj, 1024)))
                def fns(base):
                    v = views(base)
                    a, b = 128 * jlo, 128 * (jlo + nj)
                    return [lambda e: e.dma_start(out=v["wg"], in_=colsrc(wg_d[l], a, b)),
                            lambda e: e.dma_start(out=v["wu"], in_=colsrc(wu_d[l], a, b)),
                            lambda e: e.dma_start(out=v["wd"], in_=rowsrc(wd_d[l], a, b))]
            return dict(kind=kind, l=l, size=size, fns=fns, views=views, jlo=jlo, nj=nj)

        phases = []
        for u in range(NU):
            for l in range(DEPTH):
                phases += [mk_phase("P0", l), mk_phase("P1", l), mk_phase("P2a", l), mk_phase("P2b", l)]
                j = 0
                for nj in FCH:
                    phases.append(mk_phase("F", l, j, nj))
                    j += nj
        for i, ph in enumerate(phases):
            ph["par"] = i % 2
            ph["base"] = 0 if i % 2 == 0 else ARENA - ph["size"]
            ph["u"] = i // (len(phases) // NU)
        for i in range(len(phases) - 1):
            assert phases[i]["size"] + phases[i + 1]["size"] <= ARENA

        def issue_weights(i):
            if i >= len(phases):
                return
            ph = phases[i]
            P.dma("pool", "w%d" % ph["par"], ph["fns"](ph["base"]), writes=[("w", ph["par"])])

        def norm_steps(t0, n, gain, dst, dst_keys):
            b = SB
            steps = []
            def s_sq(k):
                def f():
                    P.op("act", lambda e: e.activation(out=sq[:, k % 2, :n], in_=hT[:, k, t0:t0 + n], func=AF.Square),
                         reads=hk([k], t0, n), writes=[("sq", k % 2)])
                    P.op("pe", lambda e: e.matmul(ps[:, b, :n], lhsT=ones[:, :], rhs=sq[:, k % 2, :n], start=(k == 0), stop=(k == 7)),
                         reads=[("sq", k % 2), ("ones",)], writes=[("ps", b)])
                return f
            def s_rs():
                P.op("act", lambda e: e.activation(out=ps[:, b, :n], in_=ps[:, b, :n], func=AF.Ln, bias=epsb[:], scale=1.0 / D),
                     reads=[("ps", b), ("eps",)], writes=[("ps", b)])
                P.op("act", lambda e: e.activation(out=ps[:, b, :n], in_=ps[:, b, :n], func=AF.Exp, scale=-0.5), reads=[("ps", b)], writes=[("ps", b)])
            def s_sc(k):
                def f():
                    P.op("dve", lambda e: e.scalar_tensor_tensor(out=dst(k), in0=hT[:, k, t0:t0 + n], scalar=gain[:, k:k + 1],
                                                                  in1=ps[:, b, :n], op0=ALU.mult, op1=ALU.mult),
                         reads=hk([k], t0, n) + [("ps", b), ("const",)], writes=dst_keys(k))
                return f
            for k in range(8):
                steps.append(s_sq(k))
            steps.append(s_rs)
            for k in range(8):
                steps.append(s_sc(k))
            return steps

        def norm(t0, n, gain, dst, dst_keys):
            for f in norm_steps(t0, n, gain, dst, dst_keys):
                f()

        zflat = zT[:].rearrange("p g t -> p (g t)")[:, 0:4096].rearrange("p (k t) -> p k t", k=8)
        Z_ALL = [("z", g, bb) for g in range(4) for bb in range(12)]
        def ubuf(kind, n):
            if kind == "A":
                return (lambda k: ub[:, k, :n]), (lambda k: [("u", k)]), [("u", k) for k in range(8)]
            if kind == "S":
                return (lambda k: mid[:, 3, k, :n]), (lambda k: [("mid", 3, k)]), [("mid", 3, k) for k in range(8)]
            return (lambda k: zflat[:, k, :n]), (lambda k: Z_ALL), Z_ALL

        BGN = [1]
        def proj(wv, col0, n, par, rhs_fn, rhs_keys, nk=8):
            b = nbank()
            def f(e):
                for k in range(nk):
                    ins = e.matmul(ps[:, b, :n], lhsT=wv[:, k, col0:col0 + 128], rhs=rhs_fn(k), start=(k == 0), stop=(k == nk - 1))
                return ins
            P.op("pe", f, reads=[("w", par)] + rhs_keys, writes=[("ps", b)])
            bg_step(BGN[0])
            return b

        def unit_inputs(u):
            for part in range(3):
                a0 = 512 * part
                P.dma("sp" if u == 0 else "pool", ("inp%d" if u == 0 else "inq%d") % part, [(lambda e, c=c, a0=a0: e.dma_start(out=hT[:, c, a0:a0 + 512], in_=xT_d[u, c * 128:(c + 1) * 128, a0:a0 + 512]))
                                             for c in range(8)], writes=hk(range(8), a0, 512))
                if part == 0:
                    if u == 0:
                        load_consts()
                    small_inputs(u)

        def small_inputs(u):
            P.dma("sp", "inm", [
                lambda e: e.dma_start(out=kbias[:], in_=kbias_d[u]),
                lambda e: e.dma_start(out=corr[:], in_=corr_d[u].partition_broadcast(128)),
                lambda e: e.dma_start(out=hm[:], in_=hm_d[u].partition_broadcast(128)),
            ], writes=[("kbias",), ("corr",), ("hm",)])

        def p0_tile(l, t0, n, W, par, U):
            uf, _, UK = U
            b = proj(W["kd"], 0, n, par, uf, UK)
            P.op("act", lambda e, b=b: e.activation(out=kT[:, 0, t0:t0 + n], in_=ps[:, b, :n], func=AF.Copy),
                 reads=[("ps", b)], writes=[("k", 0, bb) for bb in _blocks(t0, n)], free=[b])
            for g in range(4):
                b = proj(W["wz"], g * 128, n, par, uf, UK)
                P.op("dve", lambda e, b=b, g=g: e.tensor_copy(out=zT[:, g, t0:t0 + n], in_=ps[:, b, :n]),
                     reads=[("ps", b)], writes=[("z", g, bb) for bb in _blocks(t0, n)], free=[b])
            for i in range(n // 128):
                b = nbank()
                gb = t0 // 128 + i
                def f(e, b=b, i=i):
                    for k in range(8):
                        ins = e.matmul(ps[:, b, 0:128], lhsT=uf(k)[:, i * 128:(i + 1) * 128], rhs=W["vd"][:, k, :], start=(k == 0), stop=(k == 7))
                    return ins
                P.op("pe", f, reads=[("w", par)] + UK, writes=[("ps", b)])
                P.op("act", lambda e, b=b, gb=gb: e.activation(out=vv[:, gb, 0:128], in_=ps[:, b, 0:128], func=AF.Copy),
                     reads=[("ps", b)], writes=[("v", gb)], free=[b])
                bg_step(2)

        def pool_steps(l, slot, t0, n, W, par):
            groups = []
            tails = []
            for g, w in enumerate(POOLW):
                steps = []
                base_t = t0 - w // 2
                L = n + w - 1
                zk = [("z", g, bb) for bb in _blocks(base_t, L)]
                cur = None; curk = None
                bufs = [(pa, pa_k), (pb, pb_k)]
                s_ = 1
                bi = 0
                while s_ < w:
                    Ln = L - s_
                    ob, obk = bufs[bi]
                    if cur is None:
                        steps.append(lambda ob=ob, obk=obk, Ln=Ln, s_=s_, base_t=base_t, g=g, zk=zk: P.op("pool", lambda e: e.tensor_tensor(
                            out=ob[:, :Ln], in0=zT[:, g, base_t:base_t + Ln], in1=zT[:, g, base_t + s_:base_t + s_ + Ln], op=ALU.add),
                            reads=zk, writes=obk))
                    else:
                        steps.append(lambda ob=ob, obk=obk, cur=cur, curk=curk, Ln=Ln, s_=s_: P.op("pool", lambda e: e.tensor_tensor(
                            out=ob[:, :Ln], in0=cur[:, :Ln], in1=cur[:, s_:s_ + Ln], op=ALU.add),
                            reads=curk, writes=obk))
                    cur, curk = ob, obk
                    L = Ln
                    s_ *= 2
                    bi ^= 1
                for side, lo in ((0, 256), (1, 1272)):
                    a0 = max(lo, t0); a1 = min(lo + 8, t0 + n)
                    if a1 > a0:
                        cc = side * 32 + g * 8 + (a0 - lo)
                        steps.append(lambda cur=cur, curk=curk, a0=a0, a1=a1, cc=cc: P.op("pool", lambda e: e.tensor_tensor(
                            out=cur[:, a0 - t0:a1 - t0], in0=cur[:, a0 - t0:a1 - t0], in1=corr[:, cc:cc + (a1 - a0)], op=ALU.mult),
                            reads=curk + [("corr",)], writes=curk))
                def fin_a(cur=cur, curk=curk, g=g, w=w):
                    P.op("dve", lambda e: e.scalar_tensor_tensor(
                        out=dg[:, g, :n], in0=cur[:, :n], scalar=1.0 / w, in1=zT[:, g, t0:t0 + n], op0=ALU.mult, op1=ALU.subtract),
                        reads=curk + [("z", g, bb) for bb in _blocks(t0, n)], writes=dg_k(g))
                steps.append(fin_a)
                def fin_b(g=g):
                    b = nbank()
                    P.op("pe", lambda e: e.matmul(ps[:, b, :n], lhsT=W["wpg"][:, g, :], rhs=dg[:, g, :n], start=True, stop=True),
                         reads=[("w", par)] + dg_k(g), writes=[("ps", b)])
                    P.op("dve", lambda e: e.tensor_scalar(out=mid[:, slot, g, :n], in0=ps[:, b, :n], scalar1=pscale[:, l, g:g + 1],
                                                           scalar2=None, op0=ALU.mult),
                         reads=[("ps", b), ("const",)], writes=[("mid", slot, g)], free=[b])
                groups.append(lambda steps=steps: [f() for f in steps])
                tails.append(fin_b)
            return groups + tails

        def p1_tile(l, ti, t0, n, W, par, U, z_hazard=False):
            uf, _, UK = U
            slot = ti
            if z_hazard:
                BGN[0] = 2
            for c in range(4):
                b = proj(W["wq"], c * 128, n, par, uf, UK)
                P.op("dve", lambda e, b=b, c=c: e.tensor_scalar(out=qe[:, c, :n], in0=ps[:, b, :n], scalar1=qsc[:, 0:1], scalar2=None, op0=ALU.mult),
                     reads=[("ps", b), ("qsc",)], writes=wkk(512 * c, 512))
                P.op("dve", lambda e, b=b, c=c: e.tensor_scalar(out=qo[:, c, :n], in0=ps[:, b, :n], scalar1=qsc[:, 1:2], scalar2=None, op0=ALU.mult),
                     reads=[("ps", b), ("qsc",)], writes=wkk(2048 + 512 * c, 512), free=[b])
            bgp.extend(pool_steps(l, slot, t0, n, W, par))
            if z_hazard:
                bgp_step(16)
                BGN[0] = 3
            items = [(i, hh) for i in range(n // 128) for hh in range(2)]
            def stage_a(it):
                i, hh = it
                a = t0 + 128 * i
                bss = []
                for kb in range(3):
                    gb = a // 128 - 1 + kb
                    bs = nbank()
                    bss.append(bs)
                    def fst(e, bs=bs, kb=kb, gb=gb):
                        e.matmul(ps[:, bs, :], lhsT=ident[:, :], rhs=biasT[:, hh * 3 + kb, :], start=True, stop=False, skip_group_check=True)
                        src = qe if hh == 0 else qo
                        for g in range(4):
                            ins = e.matmul(ps[:, bs, g * 128:(g + 1) * 128], lhsT=kT[:, 0, gb * 128:(gb + 1) * 128],
                                           rhs=src[:, g, i * 128:(i + 1) * 128], start=False, stop=True, skip_group_check=True)
                        return ins
                    P.op("pe", fst, reads=[("cb",), ("k", 0, gb)] + qb_k, writes=[("ps", bs)])
                return bss
            def stage_b(it, bss):
                i, hh = it
                a = t0 + 128 * i
                bo = nbank(); bd = nbank()
                for kb in range(3):
                    gb = a // 128 - 1 + kb
                    bs = bss[kb]
                    P.op("act", lambda e, bs=bs, kb=kb, gb=gb: e.activation(out=PT[:, kb, :], in_=ps[:, bs, :], func=AF.Exp,
                                                                             bias=kbias[:, gb:gb + 1], scale=1.0),
                         reads=[("ps", bs), ("kbias",)], writes=PT_k(kb), free=[bs])
                    def fpv(e, kb=kb, gb=gb):
                        e.matmul(ps[:, bo, :], lhsT=vv[:, gb, 0:128], rhs=PT[:, kb, :], start=(kb == 0), stop=(kb == 2))
                        ins = e.matmul(ps[:, bd, :], lhsT=ones[:, :], rhs=PT[:, kb, :], start=(kb == 0), stop=False)
                        if kb == 2:
                            p0 = 32 * hh
                            ins = e.matmul(ps[:, bd, :], lhsT=ones[p0:p0 + 1, :], rhs=esrow[l][p0:p0 + 1, :], start=False, stop=True)
                        return ins
                    P.op("pe", fpv, reads=[("v", gb), ("ones",), ("esrow",)] + PT_k(kb), writes=[("ps", bo), ("ps", bd)])
                return bo, bd
            def stage_c(it, bo, bd):
                i, hh = it
                P.op("act", lambda e: e.activation(out=ps[:, bd, :], in_=ps[:, bd, :], func=AF.Ln), reads=[("ps", bd)], writes=[("ps", bd)])
                P.op("act", lambda e: e.activation(out=rden, in_=ps[:, bd, :], func=AF.Exp, scale=-1.0), reads=[("ps", bd)], writes=rden_k, free=[bd])
                psv = ps[:, bo, :].rearrange("p (g q) -> p g q", g=4)
                rdv = rden.rearrange("p (g q) -> p g q", g=4)
                r0 = 64 * hh
                P.op("dve", lambda e: e.tensor_tensor(
                    out=mid[r0:r0 + 64, slot, 4:8, i * 128:(i + 1) * 128], in0=psv[r0:r0 + 64, :, :], in1=rdv[r0:r0 + 64, :, :], op=ALU.mult),
                    reads=[("ps", bo)] + rden_k, writes=[("mid", slot, 4 + g_) for g_ in range(4)], free=[bo])
            A = {0: stage_a(items[0])}
            Bk = {0: stage_b(items[0], A[0])}
            if len(items) > 1:
                A[1] = stage_a(items[1])
            for j in range(len(items)):
                if j + 1 < len(items):
                    Bk[j + 1] = stage_b(items[j + 1], A[j + 1])
                if j + 2 < len(items):
                    A[j + 2] = stage_a(items[j + 2])
                stage_c(items[j], *Bk[j])
                bgp_step((8 + len(items) - 1) // len(items))
                bg_step(5)
            bgp_step(16)

        def p2a_tile(ti, t0, n, ms, W, par, U):
            uf, _, UK = U
            slot = ti
            for c in range(8):
                b1 = proj(W["gp"], c * 128, n, par, uf, UK)
                P.op("act", lambda e, b1=b1: e.activation(out=sgp[:, :n], in_=ps[:, b1, :n], func=AF.Sigmoid),
                     reads=[("ps", b1)], writes=sgp_k, free=[b1])
                b2 = proj(W["ga"], c * 128, n, par, uf, UK)
                P.op("act", lambda e, b2=b2: e.activation(out=sga[:, :n], in_=ps[:, b2, :n], func=AF.Sigmoid),
                     reads=[("ps", b2)], writes=sga_k, free=[b2])
                b3 = proj(W["wpb"], c * 128, n, par, lambda k: mid[:, slot, k, :n], [("mid", slot, k) for k in range(4)], nk=4)
                P.op("dve", lambda e, b3=b3: e.tensor_tensor(out=t1[:, :n], in0=sgp[:, :n], in1=ps[:, b3, :n], op=ALU.mult),
                     reads=[("ps", b3)] + sgp_k, writes=t1_k, free=[b3])
                b4 = proj(W["wab"], c * 128, n, par, lambda k: mid[:, slot, 4 + k, :n], [("mid", slot, 4 + k) for k in range(4)], nk=4)
                P.op("dve", lambda e, b4=b4: e.tensor_tensor(out=t2[:, :n], in0=sga[:, :n], in1=ps[:, b4, :n], op=ALU.mult),
                     reads=[("ps", b4)] + sga_k, writes=t2_k, free=[b4])
                P.op("dve", lambda e, c=c: e.tensor_tensor(out=mid[:, ms, c, :n], in0=t1[:, :n], in1=t2[:, :n], op=ALU.add),
                     reads=t1_k + t2_k, writes=[("mid", ms, c)])

        def p2b_tile(t0, n, ms, W, par):
            for f_ in range(8):
                b = proj(W["wo"], f_ * 128, n, par, lambda k: mid[:, ms, k, :n], [("mid", ms, k) for k in range(8)])
                P.op("dve", lambda e, b=b, f_=f_: e.tensor_tensor(out=hT[:, f_, t0:t0 + n], in0=hT[:, f_, t0:t0 + n], in1=ps[:, b, :n], op=ALU.add),
                     reads=[("ps", b)] + hk([f_], t0, n), writes=hk([f_], t0, n), free=[b])

        def p3_steps(slot, t0, n, gf):
            return norm_steps(t0, n, gf, (lambda k: mid[:, slot, k, :n]), (lambda k: [("mid", slot, k)]))

        def ffn_gate(j, n, W, par, sl):
            u2k = [("mid", sl, k) for k in range(8)]
            sb_i = j % 2
            b1 = proj(W["wg"], j * 128, n, par, lambda k: mid[:, sl, k, :n], u2k)
            P.op("act", lambda e: e.activation(out=sgt[sb_i][:, :n], in_=ps[:, b1, :n], func=AF.Silu),
                 reads=[("ps", b1)], writes=sgt_k[sb_i], free=[b1])

        def ffn_tile(ti, t0, n, nj, W, par, sl, have_g0=False, nxt=None):
            u2k = [("mid", sl, k) for k in range(8)]
            for j in range(nj):
                sb_i = j % 2
                if not (j == 0 and have_g0):
                    ffn_gate(j, n, W, par, sl)
                b2 = proj(W["wu"], j * 128, n, par, lambda k: mid[:, sl, k, :n], u2k)
                P.op("dve", lambda e, b2=b2, sb_i=sb_i, j=j: e.tensor_tensor(out=actb[:, j, :n], in0=sgt[sb_i][:, :n], in1=ps[:, b2, :n], op=ALU.mult),
                     reads=[("ps", b2)] + sgt_k[sb_i], writes=actb_k(j), free=[b2])
            if nxt is not None:
                ffn_gate(0, nxt[0], W, par, nxt[1])
            for f_ in range(8):
                b = nbank()
                def fd(e, b=b, f_=f_):
                    for j in range(nj):
                        ins = e.matmul(ps[:, b, :n], lhsT=W["wd"][:, j, f_ * 128:(f_ + 1) * 128], rhs=actb[:, j, :n], start=(j == 0), stop=(j == nj - 1))
                    return ins
                P.op("pe", fd, reads=[("w", par)] + [k_ for j in range(nj) for k_ in actb_k(j)], writes=[("ps", b)])
                P.op("dve", lambda e, b=b, f_=f_: e.tensor_tensor(out=hT[:, f_, t0:t0 + n], in0=hT[:, f_, t0:t0 + n], in1=ps[:, b, :n], op=ALU.add),
                     reads=[("ps", b)] + hk([f_], t0, n), writes=hk([f_], t0, n), free=[b])

        def final_steps(u, t0, n):
            b = SB
            steps = []
            def s_sq(k):
                def f():
                    P.op("act", lambda e: e.activation(out=sq[:, k % 2, :n], in_=hT[:, k, t0:t0 + n], func=AF.Square),
                         reads=hk([k], t0, n), writes=[("sq", k % 2)])
                    P.op("pe", lambda e: e.matmul(ps[:, b, :n], lhsT=ones[:, :], rhs=sq[:, k % 2, :n], start=(k == 0), stop=(k == 7)),
                         reads=[("sq", k % 2), ("ones",)], writes=[("ps", b)])
                return f
            def s_rs():
                P.op("act", lambda e: e.activation(out=ps[:, b, :n], in_=ps[:, b, :n], func=AF.Ln, bias=epsb[:], scale=1.0 / D),
                     reads=[("ps", b), ("eps",)], writes=[("ps", b)])
                P.op("act", lambda e: e.activation(out=ps[:, b, :n], in_=ps[:, b, :n], func=AF.Exp, scale=-0.5), reads=[("ps", b)], writes=[("ps", b)])
            def s_out(k):
                def f():
                    yb = k % 2
                    P.op("dve", lambda e: e.scalar_tensor_tensor(out=yout[yb][:, :n], in0=hT[:, k, t0:t0 + n], scalar=gfin[:, k:k + 1],
                                                                  in1=ps[:, b, :n], op0=ALU.mult, op1=ALU.mult),
                         reads=hk([k], t0, n) + [("ps", b), ("const",)], writes=yout_k[yb])
                    P.dma("sp", "outp%d" % yb, [lambda e: e.dma_start(out=out_d[u, k * 128:(k + 1) * 128, t0 - 256:t0 - 256 + n], in_=yout[yb][:, :n])],
                          reads=yout_k[yb])
                return f
            for k in range(8):
                steps.append(s_sq(k))
            steps.append(s_rs)
            for k in range(8):
                steps.append(s_out(k))
            return steps

        def issue_w(i):
            if i < _LIMIT:
                issue_weights(i)

        P.op("pool", lambda e: e.memset(ones[:], 1.0), writes=[("ones",)])
        P.op("pool", lambda e: e.memset(epsb[:], EPS), writes=[("eps",)])
        P.op("pool", lambda e: e.memset(qsc[0:64, 0:1], 0.125), writes=[("qsc",)])
        P.op("pool", lambda e: e.memset(qsc[64:128, 0:1], 0.0), writes=[("qsc",)])
        P.op("pool", lambda e: e.memset(qsc[0:64, 1:2], 0.0), writes=[("qsc",)])
        P.op("pool", lambda e: e.memset(qsc[64:128, 1:2], 0.125), writes=[("qsc",)])
        issue_w(0)
        pi = 0
        for u in range(_NU_RUN):
            unit_inputs(u)
            for l in range(DEPTH):
                if l == 0:
                    p0_tiles = [(0, 512), (512, 512), (1024, 512)]
                    main = [(128, 512), (640, 512), (1152, 256)]
                else:
                    p0_tiles = [(128, 512), (640, 512), (1152, 256)]
                    main = [(256, 512), (768, 512)]
                gm = gmix[:, l, :]
                gf = gffn[:, l, :]
                def zero_halo(side, a0):
                    for k in range(8):
                        P.op("dve", lambda e, k=k: e.tensor_scalar(out=hT[:, k, a0:a0 + 128], in0=hT[:, k, a0:a0 + 128],
                                                                   scalar1=hm[:, side:side + 1], scalar2=None, op0=ALU.mult),
                             reads=hk([k], a0, 128) + [("hm",)], writes=hk([k], a0, 128))
                k0 = ["A", "S", "A"][:len(p0_tiles)]
                k1 = ["S", "A", "S"][:len(main)]
                first2 = "Z" if k1[-1] == "A" else "A"
                k2 = [first2 if i % 2 == 0 else ("A" if first2 == "Z" else "Z") for i in range(len(main))]
                def nsteps(tile, kind):
                    t0_, n_ = tile
                    uf_, kf_, _ = ubuf(kind, n_)
                    return norm_steps(t0_, n_, gm, uf_, kf_)
                BGN[0] = 3
                ph = phases[pi]; issue_w(pi + 1); pi += 1
                if l == 0:
                    for f_ in nsteps(p0_tiles[0], k0[0]):
                        f_()
                else:
                    bg_flush()
                for ti, (t0, n) in enumerate(p0_tiles):
                    if ti + 1 < len(p0_tiles):
                        bg_add(nsteps(p0_tiles[ti + 1], k0[ti + 1]))
                    else:
                        bg_add(nsteps(main[0], k1[0]))
                    p0_tile(l, t0, n, ph["views"](ph["base"]), ph["par"], ubuf(k0[ti], n))
                    bg_flush()
                BGN[0] = 3
                ph = phases[pi]; issue_w(pi + 1); pi += 1
                for ti, (t0, n) in enumerate(main):
                    if ti + 1 < len(main):
                        bg_add(nsteps(main[ti + 1], k1[ti + 1]))
                    else:
                        bg_add(nsteps(main[0], k2[0]))
                    p1_tile(l, ti, t0, n, ph["views"](ph["base"]), ph["par"], ubuf(k1[ti], n),
                            z_hazard=(ti + 1 == len(main) and k2[0] == "Z"))
                    bg_flush()
                BGN[0] = 1
                ph = phases[pi]; issue_w(pi + 1); pi += 1
                mslot = [(ti + 3) % 4 for ti in range(len(main))]
                for ti, (t0, n) in enumerate(main):
                    if ti + 1 < len(main):
                        bg_add(nsteps(main[ti + 1], k2[ti + 1]))
                    p2a_tile(ti, t0, n, mslot[ti], ph["views"](ph["base"]), ph["par"], ubuf(k2[ti], n))
                    bg_flush()
                BGN[0] = 3
                ph = phases[pi]; issue_w(pi + 1); pi += 1
                for ti, (t0, n) in enumerate(main):
                    p2b_tile(t0, n, mslot[ti], ph["views"](ph["base"]), ph["par"])
                    bg_add(p3_steps(mslot[ti], t0, n, gf), tag=("p3", ti))
                BGN[0] = 2
                for fi, nj in enumerate(FCH):
                    ph = phases[pi]; issue_w(pi + 1); pi += 1
                    last = (fi == len(FCH) - 1)
                    for ti, (t0, n) in enumerate(main):
                        if fi == 0:
                            bg_require(("p3", ti))
                        hoist = fi > 0
                        nxt = (main[ti + 1][1], mslot[ti + 1]) if (hoist and ti + 1 < len(main)) else None
                        ffn_tile(ti, t0, n, nj, ph["views"](ph["base"]), ph["par"], mslot[ti], have_g0=(hoist and ti > 0), nxt=nxt)
                        if last and l == 0:
                            if ti == 0:
                                zero_halo(0, 128)
                                t0n, nn = 128, 512
                                uf_, kf_, _ = ubuf("A", nn)
                                bg_add(norm_steps(t0n, nn, gmix[:, 1, :], uf_, kf_))
                            if ti == len(main) - 1:
                                zero_halo(1, 1280)
                        if last and l == DEPTH - 1:
                            bg_add(final_steps(u, t0, n))
                if l < DEPTH - 1:
                    pass
            bg_flush()

        P.finalize()

        with nc.Block() as block:
            @block.tensor
            def _(e):
                P.replay("pe", e, sems)

            @block.scalar
            def _(e):
                P.replay("act", e, sems)

            @block.vector
            def _(e):
                P.replay("dve", e, sems)

            @block.gpsimd
            def _(e):
                P.replay("pool", e, sems)

            @block.sync
            def _(e):
                P.replay("sp", e, sems)
                e.wait_ge(sems["outp0"], P.totals["outp0"])
                e.wait_ge(sems["outp1"], P.totals["outp1"])
    return nc


def _consts():
    k = np.arange(128)[:, None]
    q = np.arange(128)[None, :]
    slopes = np.exp2(-8.0 * np.arange(1, 9, dtype=np.float64) / 8.0)
    biasT = np.zeros((128, 6, 4, 128), np.float32)
    for hh in range(2):
        for kb in range(3):
            dist = np.abs(q - (k + (kb - 1) * 128))
            for g in range(4):
                b = -slopes[4 * hh + g] * dist
                b = np.where(dist <= 128, b, NEG)
                biasT[:, hh * 3 + kb, g, :] = b
    return biasT.reshape(128, 3072), np.eye(128, dtype=np.float32)


def _unit_meta(s_unit):
    pos = 1024 * s_unit - HALO + np.arange(TT)
    valid = (pos >= 0) & (pos < SEQ)
    maskv = np.array([float(valid[128]), float(valid[1280])], np.float32)
    kb = np.where(valid, 0.0, NEG).astype(np.float32).reshape(12, 128).T.copy()
    corr = np.ones((2, 4, 8), np.float32)
    for g, w in enumerate(POOLW):
        for side, lo in ((0, 256), (1, 1272)):
            for i in range(8):
                p = pos[lo + i]
                if 0 <= p < SEQ:
                    lo_c = min(max(p - w // 2, 0), SEQ)
                    hi_c = min(max(p + w // 2, 0), SEQ)
                    corr[side, g, i] = float(w) / float(hi_c - lo_c)
    return maskv, kb, corr.reshape(64)


def _prep(x, norm_mix, w_in, w_pool_group, pool_scale, sink, w_pool_branch, w_attn_branch, w_out,
          norm_ffn, w_ffn_gate, w_ffn_up, w_ffn_down, norm_final, cores=range(NCORES)):
    x = np.asarray(x, np.float32)
    biasT, ident = _consts()
    f = lambda a: np.ascontiguousarray(np.asarray(a, np.float32))
    pl = lambda a: np.ascontiguousarray(np.asarray(a, np.float32).reshape(a.shape[0], -1, 128).transpose(2, 0, 1).reshape(128, -1))
    shared = dict(
        biasT=biasT, ident=ident, norm_mix=pl(norm_mix), w_in=f(w_in), w_pool_group=f(w_pool_group),
        pool_scale=pl(pool_scale), sink=f(sink).reshape(-1), w_pool_branch=f(w_pool_branch),
        w_attn_branch=f(w_attn_branch), w_out=f(w_out), norm_ffn=pl(norm_ffn), w_ffn_gate=f(w_ffn_gate),
        w_ffn_up=f(w_ffn_up), w_ffn_down=f(w_ffn_down), norm_final=pl(np.asarray(norm_final)[None, :]),
    )
    in_maps = []
    for core in cores:
        xT = np.zeros((NU, D, TT), np.float32)
        maskv = np.zeros((NU, 2), np.float32)
        kb = np.zeros((NU, 128, 12), np.float32)
        corr = np.zeros((NU, 64), np.float32)
        for uu in range(NU):
            gu = core * NU + uu
            bidx, su = gu // 4, gu % 4
            lo = 1024 * su - HALO
            a, b = max(lo, 0), min(lo + TT, SEQ)
            xT[uu, :, a - lo:b - lo] = x[bidx, a:b, :].T
            maskv[uu], kb[uu], corr[uu] = _unit_meta(su)
        m = dict(shared)
        m.update(xT=xT, hmask=maskv, kbias=kb, corr=corr)
        in_maps.append(m)
    return in_maps


def kernel(**inputs):
    in_maps = _prep(**inputs)
    nc = build_nc()
    res = run_bass_kernel_spmd(nc, in_maps, core_ids=list(range(NCORES)))
    out = np.empty((NB, SEQ, D), np.float32)
    for core in range(NCORES):
        o = res.results[core]["out"]
        for uu in range(NU):
            gu = core * NU + uu
            bidx, su = gu // 4, gu % 4
            out[bidx, 1024 * su:1024 * su + 1024, :] = o[uu].T
    return out
```

```python
import numpy as np
from contextlib import ExitStack
import concourse.bass as bass
import concourse.mybir as mybir
from concourse.bass_utils import run_bass_kernel_spmd

F32 = mybir.dt.float32
BF16 = mybir.dt.bfloat16
AF = mybir.ActivationFunctionType
ALU = mybir.AluOpType

D = 1024
SEQ = 4096
NB = 4
DEPTH = 2
DFF = 2816
INW = 3328
TT = 1536
OWN = 1024
HALO = 256
NU = 2
NCORES = 8
EPS = 1e-6
POOLW = (2, 4, 8, 16)
ARENA = 33792
FCH = (6, 5, 6, 5)
NEG = -30000.0
_LIMIT = 999
_NU_RUN = NU
_P1 = 31


class _Op:
    __slots__ = ("eng", "fn", "fns", "sem", "deps", "needed", "ticket", "waits", "stream")


class Plan:
    def __init__(self):
        self.ops = []
        self.lists = {n: [] for n in ("pe", "act", "dve", "pool", "sp")}
        self.last_w = {}
        self.readers = {}

    def _add(self, op, reads, writes):
        oid = len(self.ops)
        deps = {}
        def dep(d):
            s = self.ops[d].stream
            if deps.get(s, -1) < d:
                deps[s] = d
        for k in reads:
            d = self.last_w.get(k)
            if d is not None:
                dep(d)
        for k in writes:
            d = self.last_w.get(k)
            if d is not None:
                dep(d)
            for d in self.readers.get(k, {}).values():
                dep(d)
        for k in writes:
            self.last_w[k] = oid
            self.readers[k] = {}
        for k in reads:
            self.readers.setdefault(k, {})[op.stream] = oid
        if op.eng == "pe":
            deps.pop("pe", None)
        op.deps = list(deps.values())
        op.needed = False
        self.ops.append(op)
        self.lists[op.eng].append(op)
        return oid

    def op(self, eng, fn, reads=(), writes=(), free=()):
        o = _Op()
        o.eng = eng; o.fn = fn; o.fns = None; o.sem = None; o.stream = eng
        r = self._add(o, reads, writes)
        for b in free:
            self.free_hook(b)
        return r

    def dma(self, eng, sem, fns, reads=(), writes=()):
        o = _Op()
        o.eng = eng; o.fn = None; o.fns = list(fns); o.sem = sem; o.stream = sem
        return self._add(o, reads, writes)

    def finalize(self):
        for o in self.ops:
            for d in o.deps:
                self.ops[d].needed = True
        cnt = {}
        for o in self.ops:
            if o.fns is not None:
                cnt[o.sem] = cnt.get(o.sem, 0) + 16 * len(o.fns)
                o.ticket = cnt[o.sem]
            elif o.needed:
                cnt[o.eng] = cnt.get(o.eng, 0) + 1
                o.ticket = cnt[o.eng]
        for o in self.ops:
            o.waits = [(self.ops[d].stream, self.ops[d].ticket) for d in o.deps]
        self.totals = cnt

    def replay(self, name, e, sems):
        waited = {}
        for o in self.lists[name]:
            for (s, v) in o.waits:
                if waited.get(s, 0) < v:
                    e.wait_ge(sems[s], v)
                    waited[s] = v
            if o.fns is not None:
                for f in o.fns:
                    f(e).then_inc(sems[o.sem], 16)
            else:
                ins = o.fn(e)
                if o.needed:
                    ins.then_inc(sems[name], 1)


def _blocks(t0, n):
    return range(t0 // 128, (t0 + n - 1) // 128 + 1)


def build_nc():
    nc = bass.Bass("TRN2", target_bir_lowering=False)
    dt = nc.dram_tensor
    xT_d = dt("xT", [NU, D, TT], F32, kind="ExternalInput").ap()
    hm_d = dt("hmask", [NU, 2], F32, kind="ExternalInput").ap()
    kbias_d = dt("kbias", [NU, 128, 12], F32, kind="ExternalInput").ap()
    corr_d = dt("corr", [NU, 64], F32, kind="ExternalInput").ap()
    biasT_d = dt("biasT", [128, 3072], F32, kind="ExternalInput").ap()
    ident_d = dt("ident", [128, 128], F32, kind="ExternalInput").ap()
    norm_mix_d = dt("norm_mix", [128, DEPTH * 8], F32, kind="ExternalInput").ap()
    w_in_d = dt("w_in", [DEPTH, D, INW], F32, kind="ExternalInput").ap()
    wpg_d = dt("w_pool_group", [DEPTH, 4, 128, 128], F32, kind="ExternalInput").ap()
    pscale_d = dt("pool_scale", [128, DEPTH * 4], F32, kind="ExternalInput").ap()
    sink_d = dt("sink", [DEPTH * 8], F32, kind="ExternalInput").ap()
    wpb_d = dt("w_pool_branch", [DEPTH, 512, D], F32, kind="ExternalInput").ap()
    wab_d = dt("w_attn_branch", [DEPTH, 512, D], F32, kind="ExternalInput").ap()
    wout_d = dt("w_out", [DEPTH, D, D], F32, kind="ExternalInput").ap()
    norm_ffn_d = dt("norm_ffn", [128, DEPTH * 8], F32, kind="ExternalInput").ap()
    wg_d = dt("w_ffn_gate", [DEPTH, D, DFF], F32, kind="ExternalInput").ap()
    wu_d = dt("w_ffn_up", [DEPTH, D, DFF], F32, kind="ExternalInput").ap()
    wd_d = dt("w_ffn_down", [DEPTH, DFF, D], F32, kind="ExternalInput").ap()
    norm_final_d = dt("norm_final", [128, 8], F32, kind="ExternalInput").ap()
    out_d = dt("out", [NU, D, OWN], F32, kind="ExternalOutput").ap()

    P = Plan()
    with ExitStack() as es:
        sb = lambda name, shape, dty: es.enter_context(nc.sbuf_tensor(name, shape, dty))
        hT = sb("hT", [128, 8, TT], F32)
        mid = sb("mid", [128, 4, 8, 512], BF16)
        kT = sb("kT", [128, 1, TT], BF16)
        vv = sb("vv", [128, 12, 128], BF16)
        zT = sb("zT", [128, 4, TT], BF16)
        hm = sb("hm", [128, 2], F32)
        ub = sb("ub", [128, 8, 512], BF16)
        sq = sb("sq", [128, 2, 512], BF16)
        arena = sb("arena", [128, ARENA], BF16)
        wk = sb("wk", [128, 9792], BF16)
        biasT = sb("biasTs", [128, 6, 512], BF16)
        ident = sb("idents", [128, 128], BF16)
        ones = sb("ones", [128, 128], BF16)
        gmix = sb("gmix", [128, DEPTH, 8], F32)
        gffn = sb("gffn", [128, DEPTH, 8], F32)
        gfin = sb("gfin", [128, 8], F32)
        pscale = sb("pscale", [128, DEPTH, 4], F32)
        esk = sb("esk", [128, 16], F32)
        esrow = [sb("esrow0", [128, 512], BF16), sb("esrow1", [128, 512], BF16)]
        kbias = sb("kbiass", [128, 12], F32)
        corr = sb("corrs", [128, 64], F32)
        epsb = sb("epsb", [128, 1], F32)
        qsc = sb("qsc", [128, 2], F32)
        rden_t = sb("rden", [128, 512], BF16)
        rden = rden_t[:, :]; rden_k = [("rden",)]
        ps = es.enter_context(nc.psum_tensor("ps", [128, 8, 512], F32))
        sem_names = ["pe", "act", "dve", "pool", "w0", "w1", "inp0", "inp1", "inp2", "inq0", "inq1", "inq2", "inc", "inm", "cb", "outp0", "outp1"]
        sems = {n: es.enter_context(nc.semaphore("s_" + n)) for n in sem_names}

        def wkv(start, length, dty=BF16):
            v = wk[:, start:start + length]
            return v.bitcast(dty) if dty != BF16 else v
        def wkk(start, length):
            return [("wk", b) for b in range(start // 32, (start + length - 1) // 32 + 1)]
        qe = wkv(0, 2048).rearrange("p (c t) -> p c t", c=4)
        qo = wkv(2048, 2048).rearrange("p (c t) -> p c t", c=4)
        qb_k = wkk(0, 4096)
        dg = wkv(4096, 2048).rearrange("p (c t) -> p c t", c=4)
        def dg_k(g): return wkk(4096 + 512 * g, 512)
        PT = wkv(8256, 1536).rearrange("p (c t) -> p c t", c=3)
        def PT_k(kb): return wkk(8256 + 512 * kb, 512)
        pa = wkv(6144, 1056, F32); pa_k = wkk(6144, 1056)
        pb = wkv(7200, 1056, F32); pb_k = wkk(7200, 1056)
        sgp = wkv(0, 1024, F32); sgp_k = wkk(0, 1024)
        sga = wkv(1024, 1024, F32); sga_k = wkk(1024, 1024)
        t1 = wkv(2048, 1024, F32); t1_k = wkk(2048, 1024)
        t2 = wkv(3072, 1024, F32); t2_k = wkk(3072, 1024)
        sgt = [wkv(0, 1024, F32), wkv(1024, 1024, F32)]
        sgt_k = [wkk(0, 1024), wkk(1024, 1024)]
        actb = wkv(2048, 3072).rearrange("p (c t) -> p c t", c=6)
        def actb_k(j): return wkk(2048 + 512 * j, 512)
        yout = [wkv(5632, 1024, F32), wkv(6656, 1024, F32)]
        yout_k = [wkk(5632, 1024), wkk(6656, 1024)]

        bank_ctr = [0]
        free_banks = list(range(7))
        def nbank():
            assert free_banks, "out of PSUM banks"
            return free_banks.pop(0)
        def bfree(b):
            assert b not in free_banks and 0 <= b < 7
            free_banks.append(b)
        P.free_hook = bfree
        SB = 7

        bg = []
        def bg_add(steps, tag=None):
            bg.extend((tag, f) for f in steps)
        def bg_step(k=1):
            for _ in range(k):
                if bg:
                    bg.pop(0)[1]()
        def bg_flush():
            while bg:
                bg.pop(0)[1]()
        bgp = []
        def bgp_step(k=1):
            for _ in range(k):
                if bgp:
                    bgp.pop(0)()
        def bg_require(tag):
            while any(t == tag for t, _ in bg):
                bg.pop(0)[1]()

        def hk(cs, t0, n):
            return [("h", c, b) for c in cs for b in _blocks(t0, n)]

        def load_consts():
            P.dma("sp", "inc", [
                lambda e: e.dma_start(out=gmix[:], in_=norm_mix_d.rearrange("p (l c) -> p l c", l=DEPTH)),
                lambda e: e.dma_start(out=gffn[:], in_=norm_ffn_d.rearrange("p (l c) -> p l c", l=DEPTH)),
                lambda e: e.dma_start(out=gfin[:], in_=norm_final_d),
                lambda e: e.dma_start(out=pscale[:], in_=pscale_d.rearrange("p (l c) -> p l c", l=DEPTH)),
                lambda e: e.dma_start(out=esk[:], in_=sink_d.partition_broadcast(128)),
            ], writes=[("const",), ("esk",)])
            P.dma("pool", "cb", [
                lambda e: e.dma_start(out=biasT[:], in_=biasT_d.rearrange("p (c t) -> p c t", c=6)),
                lambda e: e.dma_start(out=ident[:], in_=ident_d),
            ], writes=[("cb",)])
            P.op("act", lambda e: e.activation(out=esk[:], in_=esk[:], func=AF.Exp), reads=[("esk",)], writes=[("esk",)])
            for l_ in range(DEPTH):
                for hh_ in range(2):
                    for g_ in range(4):
                        p0_ = 32 * hh_
                        ix_ = l_ * 8 + 4 * hh_ + g_
                        P.op("dve", lambda e, l_=l_, p0_=p0_, g_=g_, ix_=ix_: e.tensor_copy(
                            out=esrow[l_][p0_:p0_ + 1, g_ * 128:(g_ + 1) * 128], in_=esk[p0_:p0_ + 1, ix_:ix_ + 1].to_broadcast([1, 128])),
                            reads=[("esk",)], writes=[("esrow",)])


        phase_ctr = [0]

        def wview(off, shape3):
            n = shape3[0] * shape3[1]
            return arena[:, off:off + n].rearrange("p (c f) -> p c f", c=shape3[0])

        def colsrc(w2d, a, b):
            return w2d[:, a:b].rearrange("(c p) f -> p c f", p=128)

        def rowsrc(w2d, a, b):
            return w2d[a:b, :].rearrange("(c p) f -> p c f", p=128)

        def mk_phase(kind, l, jlo=0, nj=0):
            if kind == "P0":
                size = 6 * 1024
                def views(base):
                    return dict(kd=wview(base, (8, 128)), vd=wview(base + 1024, (8, 128)), wz=wview(base + 2048, (8, 512)))
                def fns(base):
                    v = views(base)
                    return [lambda e: e.dma_start(out=v["kd"], in_=colsrc(w_in_d[l], 1024, 1152)),
                            lambda e: e.dma_start(out=v["vd"], in_=colsrc(w_in_d[l], 1152, 1280)),
                            lambda e: e.dma_start(out=v["wz"], in_=colsrc(w_in_d[l], 0, 512))]
            elif kind == "P1":
                size = 4096 + 512
                def views(base):
                    return dict(wq=wview(base, (8, 512)), wpg=wview(base + 4096, (4, 128)))
                def fns(base):
                    v = views(base)
                    wqv = v["wq"].rearrange("p k (g h d) -> p k g h d", g=4, h=2)
                    out = []
                    for h_ in range(2):
                        for g_ in range(4):
                            c0 = 512 + 256 * h_ + 64 * g_
                            out.append(lambda e, h_=h_, g_=g_, c0=c0: e.dma_start(out=wqv[:, :, g_, h_, :], in_=colsrc(w_in_d[l], c0, c0 + 64)))
                    out.append(lambda e: e.dma_start(out=v["wpg"], in_=wpg_d[l].rearrange("g p f -> p g f")))
                    return out
            elif kind == "P2a":
                size = 24576
                def views(base):
                    return dict(gp=wview(base, (8, 1024)), ga=wview(base + 8192, (8, 1024)),
                                wpb=wview(base + 16384, (4, 1024)), wab=wview(base + 20480, (4, 1024)))
                def fns(base):
                    v = views(base)
                    return [lambda e: e.dma_start(out=v["gp"], in_=colsrc(w_in_d[l], 1280, 2304)),
                            lambda e: e.dma_start(out=v["wpb"], in_=rowsrc(wpb_d[l], 0, 512)),
                            lambda e: e.dma_start(out=v["ga"], in_=colsrc(w_in_d[l], 2304, 3328)),
                            lambda e: e.dma_start(out=v["wab"][0:64, :, :], in_=wab_d[l][0:256, :].rearrange("(g d) f -> d g f", d=64)),
                            lambda e: e.dma_start(out=v["wab"][64:128, :, :], in_=wab_d[l][256:512, :].rearrange("(g d) f -> d g f", d=64))]
            elif kind == "P2b":
                size = 8192
                def views(base):
                    return dict(wo=wview(base, (8, 1024)))
                def fns(base):
                    v = views(base)
                    return [lambda e: e.dma_start(out=v["wo"], in_=colsrc(wout_d[l], 0, 1024))]
            elif kind == "F":
                size = nj * 3072
                def views(base):
                    return dict(wg=wview(base, (8, 128 * nj)), wu=wview(base + 1024 * nj, (8, 128 * nj)),
                                wd=wview(base + 2048 * nj, (nj, 1024)))
                def fns(base):
                    v = views(base)
                    a, b = 128 * jlo, 128 * (jlo + nj)
                    return [lambda e: e.dma_start(out=v["wg"], in_=colsrc(wg_d[l], a, b)),
                            lambda e: e.dma_start(out=v["wu"], in_=colsrc(wu_d[l], a, b)),
                            lambda e: e.dma_start(out=v["wd"], in_=rowsrc(wd_d[l], a, b))]
            return dict(kind=kind, l=l, size=size, fns=fns, views=views, jlo=jlo, nj=nj)

        phases = []
        for u in range(NU):
            for l in range(DEPTH):
                phases += [mk_phase("P0", l), mk_phase("P1", l), mk_phase("P2a", l), mk_phase("P2b", l)]
                j = 0
                for nj in FCH:
                    phases.append(mk_phase("F", l, j, nj))
                    j += nj
        for i, ph in enumerate(phases):
            ph["par"] = i % 2
            ph["base"] = 0 if i % 2 == 0 else ARENA - ph["size"]
            ph["u"] = i // (len(phases) // NU)
        for i in range(len(phases) - 1):
            assert phases[i]["size"] + phases[i + 1]["size"] <= ARENA

        def issue_weights(i):
            if i >= len(phases):
                return
            ph = phases[i]
            P.dma("pool", "w%d" % ph["par"], ph["fns"](ph["base"]), writes=[("w", ph["par"])])

        def norm_steps(t0, n, gain, dst, dst_keys):
            b = SB
            steps = []
            def s_sq(k):
                def f():
                    P.op("act", lambda e: e.activation(out=sq[:, k % 2, :n], in_=hT[:, k, t0:t0 + n], func=AF.Square),
                         reads=hk([k], t0, n), writes=[("sq", k % 2)])
                    P.op("pe", lambda e: e.matmul(ps[:, b, :n], lhsT=ones[:, :], rhs=sq[:, k % 2, :n], start=(k == 0), stop=(k == 7)),
                         reads=[("sq", k % 2), ("ones",)], writes=[("ps", b)])
                return f
            def s_rs():
                P.op("act", lambda e: e.activation(out=ps[:, b, :n], in_=ps[:, b, :n], func=AF.Ln, bias=epsb[:], scale=1.0 / D),
                     reads=[("ps", b), ("eps",)], writes=[("ps", b)])
                P.op("act", lambda e: e.activation(out=ps[:, b, :n], in_=ps[:, b, :n], func=AF.Exp, scale=-0.5), reads=[("ps", b)], writes=[("ps", b)])
            def s_sc(k):
                def f():
                    P.op("dve", lambda e: e.scalar_tensor_tensor(out=dst(k), in0=hT[:, k, t0:t0 + n], scalar=gain[:, k:k + 1],
                                                                  in1=ps[:, b, :n], op0=ALU.mult, op1=ALU.mult),
                         reads=hk([k], t0, n) + [("ps", b), ("const",)], writes=dst_keys(k))
                return f
            for k in range(8):
                steps.append(s_sq(k))
            steps.append(s_rs)
            for k in range(8):
                steps.append(s_sc(k))
            return steps

        def norm(t0, n, gain, dst, dst_keys):
            for f in norm_steps(t0, n, gain, dst, dst_keys):
                f()

        zflat = zT[:].rearrange("p g t -> p (g t)")[:, 0:4096].rearrange("p (k t) -> p k t", k=8)
        Z_ALL = [("z", g, bb) for g in range(4) for bb in range(12)]
        def ubuf(kind, n):
            if kind == "A":
                return (lambda k: ub[:, k, :n]), (lambda k: [("u", k)]), [("u", k) for k in range(8)]
            if kind == "S":
                return (lambda k: mid[:, 3, k, :n]), (lambda k: [("mid", 3, k)]), [("mid", 3, k) for k in range(8)]
            return (lambda k: zflat[:, k, :n]), (lambda k: Z_ALL), Z_ALL

        BGN = [1]
        def proj(wv, col0, n, par, rhs_fn, rhs_keys, nk=8):
            b = nbank()
            def f(e):
                for k in range(nk):
                    ins = e.matmul(ps[:, b, :n], lhsT=wv[:, k, col0:col0 + 128], rhs=rhs_fn(k), start=(k == 0), stop=(k == nk - 1))
                return ins
            P.op("pe", f, reads=[("w", par)] + rhs_keys, writes=[("ps", b)])
            bg_step(BGN[0])
            return b

        def unit_inputs(u):
            for part in range(3):
                a0 = 512 * part
                P.dma("sp" if u == 0 else "pool", ("inp%d" if u == 0 else "inq%d") % part, [(lambda e, c=c, a0=a0: e.dma_start(out=hT[:, c, a0:a0 + 512], in_=xT_d[u, c * 128:(c + 1) * 128, a0:a0 + 512]))
                                             for c in range(8)], writes=hk(range(8), a0, 512))
                if part == 0:
                    if u == 0:
                        load_consts()
                    small_inputs(u)

        def small_inputs(u):
            P.dma("sp", "inm", [
                lambda e: e.dma_start(out=kbias[:], in_=kbias_d[u]),
                lambda e: e.dma_start(out=corr[:], in_=corr_d[u].partition_broadcast(128)),
                lambda e: e.dma_start(out=hm[:], in_=hm_d[u].partition_broadcast(128)),
            ], writes=[("kbias",), ("corr",), ("hm",)])

        def p0_tile(l, t0, n, W, par, U):
            uf, _, UK = U
            b = proj(W["kd"], 0, n, par, uf, UK)
            P.op("act", lambda e, b=b: e.activation(out=kT[:, 0, t0:t0 + n], in_=ps[:, b, :n], func=AF.Copy),
                 reads=[("ps", b)], writes=[("k", 0, bb) for bb in _blocks(t0, n)], free=[b])
            for g in range(4):
                b = proj(W["wz"], g * 128, n, par, uf, UK)
                P.op("dve", lambda e, b=b, g=g: e.tensor_copy(out=zT[:, g, t0:t0 + n], in_=ps[:, b, :n]),
                     reads=[("ps", b)], writes=[("z", g, bb) for bb in _blocks(t0, n)], free=[b])
            for i in range(n // 128):
                b = nbank()
                gb = t0 // 128 + i
                def f(e, b=b, i=i):
                    for k in range(8):
                        ins = e.matmul(ps[:, b, 0:128], lhsT=uf(k)[:, i * 128:(i + 1) * 128], rhs=W["vd"][:, k, :], start=(k == 0), stop=(k == 7))
                    return ins
                P.op("pe", f, reads=[("w", par)] + UK, writes=[("ps", b)])
                P.op("act", lambda e, b=b, gb=gb: e.activation(out=vv[:, gb, 0:128], in_=ps[:, b, 0:128], func=AF.Copy),
                     reads=[("ps", b)], writes=[("v", gb)], free=[b])
                bg_step(2)

        def pool_steps(l, slot, t0, n, W, par):
            groups = []
            tails = []
            for g, w in enumerate(POOLW):
                steps = []
                base_t = t0 - w // 2
                L = n + w - 1
                zk = [("z", g, bb) for bb in _blocks(base_t, L)]
                cur = None; curk = None
                bufs = [(pa, pa_k), (pb, pb_k)]
                s_ = 1
                bi = 0
                while s_ < w:
                    Ln = L - s_
                    ob, obk = bufs[bi]
                    if cur is None:
                        steps.append(lambda ob=ob, obk=obk, Ln=Ln, s_=s_, base_t=base_t, g=g, zk=zk: P.op("pool", lambda e: e.tensor_tensor(
                            out=ob[:, :Ln], in0=zT[:, g, base_t:base_t + Ln], in1=zT[:, g, base_t + s_:base_t + s_ + Ln], op=ALU.add),
                            reads=zk, writes=obk))
                    else:
                        steps.append(lambda ob=ob, obk=obk, cur=cur, curk=curk, Ln=Ln, s_=s_: P.op("pool", lambda e: e.tensor_tensor(
                            out=ob[:, :Ln], in0=cur[:, :Ln], in1=cur[:, s_:s_ + Ln], op=ALU.add),
                            reads=curk, writes=obk))
                    cur, curk = ob, obk
                    L = Ln
                    s_ *= 2
                    bi ^= 1
                for side, lo in ((0, 256), (1, 1272)):
                    a0 = max(lo, t0); a1 = min(lo + 8, t0 + n)
                    if a1 > a0:
                        cc = side * 32 + g * 8 + (a0 - lo)
                        steps.append(lambda cur=cur, curk=curk, a0=a0, a1=a1, cc=cc: P.op("pool", lambda e: e.tensor_tensor(
                            out=cur[:, a0 - t0:a1 - t0], in0=cur[:, a0 - t0:a1 - t0], in1=corr[:, cc:cc + (a1 - a0)], op=ALU.mult),
                            reads=curk + [("corr",)], writes=curk))
                def fin_a(cur=cur, curk=curk, g=g, w=w):
                    P.op("dve", lambda e: e.scalar_tensor_tensor(
                        out=dg[:, g, :n], in0=cur[:, :n], scalar=1.0 / w, in1=zT[:, g, t0:t0 + n], op0=ALU.mult, op1=ALU.subtract),
                        reads=curk + [("z", g, bb) for bb in _blocks(t0, n)], writes=dg_k(g))
                steps.append(fin_a)
                def fin_b(g=g):
                    b = nbank()
                    P.op("pe", lambda e: e.matmul(ps[:, b, :n], lhsT=W["wpg"][:, g, :], rhs=dg[:, g, :n], start=True, stop=True),
                         reads=[("w", par)] + dg_k(g), writes=[("ps", b)])
                    P.op("dve", lambda e: e.tensor_scalar(out=mid[:, slot, g, :n], in0=ps[:, b, :n], scalar1=pscale[:, l, g:g + 1],
                                                           scalar2=None, op0=ALU.mult),
                         reads=[("ps", b), ("const",)], writes=[("mid", slot, g)], free=[b])
                groups.append(lambda steps=steps: [f() for f in steps])
                tails.append(fin_b)
            return groups + tails

        def p1_tile(l, ti, t0, n, W, par, U, z_hazard=False):
            uf, _, UK = U
            slot = ti
            if z_hazard:
                BGN[0] = 2
            for c in range(4):
                b = proj(W["wq"], c * 128, n, par, uf, UK)
                P.op("dve", lambda e, b=b, c=c: e.tensor_scalar(out=qe[:, c, :n], in0=ps[:, b, :n], scalar1=qsc[:, 0:1], scalar2=None, op0=ALU.mult),
                     reads=[("ps", b), ("qsc",)], writes=wkk(512 * c, 512))
                P.op("dve", lambda e, b=b, c=c: e.tensor_scalar(out=qo[:, c, :n], in0=ps[:, b, :n], scalar1=qsc[:, 1:2], scalar2=None, op0=ALU.mult),
                     reads=[("ps", b), ("qsc",)], writes=wkk(2048 + 512 * c, 512), free=[b])
            bgp.extend(pool_steps(l, slot, t0, n, W, par))
            if z_hazard:
                bgp_step(16)
                BGN[0] = 3
            items = [(i, hh) for i in range(n // 128) for hh in range(2)]
            def stage_a(it):
                i, hh = it
                a = t0 + 128 * i
                bss = []
                for kb in range(3):
                    gb = a // 128 - 1 + kb
                    bs = nbank()
                    bss.append(bs)
                    def fst(e, bs=bs, kb=kb, gb=gb):
                        e.matmul(ps[:, bs, :], lhsT=ident[:, :], rhs=biasT[:, hh * 3 + kb, :], start=True, stop=False, skip_group_check=True)
                        src = qe if hh == 0 else qo
                        for g in range(4):
                            ins = e.matmul(ps[:, bs, g * 128:(g + 1) * 128], lhsT=kT[:, 0, gb * 128:(gb + 1) * 128],
                                           rhs=src[:, g, i * 128:(i + 1) * 128], start=False, stop=True, skip_group_check=True)
                        return ins
                    P.op("pe", fst, reads=[("cb",), ("k", 0, gb)] + qb_k, writes=[("ps", bs)])
                return bss
            def stage_b(it, bss):
                i, hh = it
                a = t0 + 128 * i
                bo = nbank(); bd = nbank()
                for kb in range(3):
                    gb = a // 128 - 1 + kb
                    bs = bss[kb]
                    P.op("act", lambda e, bs=bs, kb=kb, gb=gb: e.activation(out=PT[:, kb, :], in_=ps[:, bs, :], func=AF.Exp,
                                                                             bias=kbias[:, gb:gb + 1], scale=1.0),
                         reads=[("ps", bs), ("kbias",)], writes=PT_k(kb), free=[bs])
                    def fpv(e, kb=kb, gb=gb):
                        e.matmul(ps[:, bo, :], lhsT=vv[:, gb, 0:128], rhs=PT[:, kb, :], start=(kb == 0), stop=(kb == 2))
                        ins = e.matmul(ps[:, bd, :], lhsT=ones[:, :], rhs=PT[:, kb, :], start=(kb == 0), stop=False)
                        if kb == 2:
                            p0 = 32 * hh
                            ins = e.matmul(ps[:, bd, :], lhsT=ones[p0:p0 + 1, :], rhs=esrow[l][p0:p0 + 1, :], start=False, stop=True)
                        return ins
                    P.op("pe", fpv, reads=[("v", gb), ("ones",), ("esrow",)] + PT_k(kb), writes=[("ps", bo), ("ps", bd)])
                return bo, bd
            def stage_c(it, bo, bd):
                i, hh = it
                P.op("act", lambda e: e.activation(out=ps[:, bd, :], in_=ps[:, bd, :], func=AF.Ln), reads=[("ps", bd)], writes=[("ps", bd)])
                P.op("act", lambda e: e.activation(out=rden, in_=ps[:, bd, :], func=AF.Exp, scale=-1.0), reads=[("ps", bd)], writes=rden_k, free=[bd])
                psv = ps[:, bo, :].rearrange("p (g q) -> p g q", g=4)
                rdv = rden.rearrange("p (g q) -> p g q", g=4)
                r0 = 64 * hh
                P.op("dve", lambda e: e.tensor_tensor(
                    out=mid[r0:r0 + 64, slot, 4:8, i * 128:(i + 1) * 128], in0=psv[r0:r0 + 64, :, :], in1=rdv[r0:r0 + 64, :, :], op=ALU.mult),
                    reads=[("ps", bo)] + rden_k, writes=[("mid", slot, 4 + g_) for g_ in range(4)], free=[bo])
            A = {0: stage_a(items[0])}
            Bk = {0: stage_b(items[0], A[0])}
            if len(items) > 1:
                A[1] = stage_a(items[1])
            for j in range(len(items)):
                if j + 1 < len(items):
                    Bk[j + 1] = stage_b(items[j + 1], A[j + 1])
                if j + 2 < len(items):
                    A[j + 2] = stage_a(items[j + 2])
                stage_c(items[j], *Bk[j])
                bgp_step((8 + len(items) - 1) // len(items))
                bg_step(5)
            bgp_step(16)

        def p2a_tile(ti, t0, n, ms, W, par, U):
            uf, _, UK = U
            slot = ti
            for c in range(8):
                b1 = proj(W["gp"], c * 128, n, par, uf, UK)
                P.op("act", lambda e, b1=b1: e.activation(out=sgp[:, :n], in_=ps[:, b1, :n], func=AF.Sigmoid),
                     reads=[("ps", b1)], writes=sgp_k, free=[b1])
                b2 = proj(W["ga"], c * 128, n, par, uf, UK)
                P.op("act", lambda e, b2=b2: e.activation(out=sga[:, :n], in_=ps[:, b2, :n], func=AF.Sigmoid),
                     reads=[("ps", b2)], writes=sga_k, free=[b2])
                b3 = proj(W["wpb"], c * 128, n, par, lambda k: mid[:, slot, k, :n], [("mid", slot, k) for k in range(4)], nk=4)
                P.op("dve", lambda e, b3=b3: e.tensor_tensor(out=t1[:, :n], in0=sgp[:, :n], in1=ps[:, b3, :n], op=ALU.mult),
                     reads=[("ps", b3)] + sgp_k, writes=t1_k, free=[b3])
                b4 = proj(W["wab"], c * 128, n, par, lambda k: mid[:, slot, 4 + k, :n], [("mid", slot, 4 + k) for k in range(4)], nk=4)
                P.op("dve", lambda e, b4=b4: e.tensor_tensor(out=t2[:, :n], in0=sga[:, :n], in1=ps[:, b4, :n], op=ALU.mult),
                     reads=[("ps", b4)] + sga_k, writes=t2_k, free=[b4])
                P.op("dve", lambda e, c=c: e.tensor_tensor(out=mid[:, ms, c, :n], in0=t1[:, :n], in1=t2[:, :n], op=ALU.add),
                     reads=t1_k + t2_k, writes=[("mid", ms, c)])

        def p2b_tile(t0, n, ms, W, par):
            for f_ in range(8):
                b = proj(W["wo"], f_ * 128, n, par, lambda k: mid[:, ms, k, :n], [("mid", ms, k) for k in range(8)])
                P.op("dve", lambda e, b=b, f_=f_: e.tensor_tensor(out=hT[:, f_, t0:t0 + n], in0=hT[:, f_, t0:t0 + n], in1=ps[:, b, :n], op=ALU.add),
                     reads=[("ps", b)] + hk([f_], t0, n), writes=hk([f_], t0, n), free=[b])

        def p3_steps(slot, t0, n, gf):
            return norm_steps(t0, n, gf, (lambda k: mid[:, slot, k, :n]), (lambda k: [("mid", slot, k)]))

        def ffn_gate(j, n, W, par, sl):
            u2k = [("mid", sl, k) for k in range(8)]
            sb_i = j % 2
            b1 = proj(W["wg"], j * 128, n, par, lambda k: mid[:, sl, k, :n], u2k)
            P.op("act", lambda e: e.activation(out=sgt[sb_i][:, :n], in_=ps[:, b1, :n], func=AF.Silu),
                 reads=[("ps", b1)], writes=sgt_k[sb_i], free=[b1])

        def ffn_tile(ti, t0, n, nj, W, par, sl, have_g0=False, nxt=None):
            u2k = [("mid", sl, k) for k in range(8)]
            for j in range(nj):
                sb_i = j % 2
                if not (j == 0 and have_g0):
                    ffn_gate(j, n, W, par, sl)
                b2 = proj(W["wu"], j * 128, n, par, lambda k: mid[:, sl, k, :n], u2k)
                P.op("dve", lambda e, b2=b2, sb_i=sb_i, j=j: e.tensor_tensor(out=actb[:, j, :n], in0=sgt[sb_i][:, :n], in1=ps[:, b2, :n], op=ALU.mult),
                     reads=[("ps", b2)] + sgt_k[sb_i], writes=actb_k(j), free=[b2])
            if nxt is not None:
                ffn_gate(0, nxt[0], W, par, nxt[1])
            for f_ in range(8):
                b = nbank()
                def fd(e, b=b, f_=f_):
                    for j in range(nj):
                        ins = e.matmul(ps[:, b, :n], lhsT=W["wd"][:, j, f_ * 128:(f_ + 1) * 128], rhs=actb[:, j, :n], start=(j == 0), stop=(j == nj - 1))
                    return ins
                P.op("pe", fd, reads=[("w", par)] + [k_ for j in range(nj) for k_ in actb_k(j)], writes=[("ps", b)])
                P.op("dve", lambda e, b=b, f_=f_: e.tensor_tensor(out=hT[:, f_, t0:t0 + n], in0=hT[:, f_, t0:t0 + n], in1=ps[:, b, :n], op=ALU.add),
                     reads=[("ps", b)] + hk([f_], t0, n), writes=hk([f_], t0, n), free=[b])

        def final_steps(u, t0, n):
            b = SB
            steps = []
            def s_sq(k):
                def f():
                    P.op("act", lambda e: e.activation(out=sq[:, k % 2, :n], in_=hT[:, k, t0:t0 + n], func=AF.Square),
                         reads=hk([k], t0, n), writes=[("sq", k % 2)])
                    P.op("pe", lambda e: e.matmul(ps[:, b, :n], lhsT=ones[:, :], rhs=sq[:, k % 2, :n], start=(k == 0), stop=(k == 7)),
                         reads=[("sq", k % 2), ("ones",)], writes=[("ps", b)])
                return f
            def s_rs():
                P.op("act", lambda e: e.activation(out=ps[:, b, :n], in_=ps[:, b, :n], func=AF.Ln, bias=epsb[:], scale=1.0 / D),
                     reads=[("ps", b), ("eps",)], writes=[("ps", b)])
                P.op("act", lambda e: e.activation(out=ps[:, b, :n], in_=ps[:, b, :n], func=AF.Exp, scale=-0.5), reads=[("ps", b)], writes=[("ps", b)])
            def s_out(k):
                def f():
                    yb = k % 2
                    P.op("dve", lambda e: e.scalar_tensor_tensor(out=yout[yb][:, :n], in0=hT[:, k, t0:t0 + n], scalar=gfin[:, k:k + 1],
                                                                  in1=ps[:, b, :n], op0=ALU.mult, op1=ALU.mult),
                         reads=hk([k], t0, n) + [("ps", b), ("const",)], writes=yout_k[yb])
                    P.dma("sp", "outp%d" % yb, [lambda e: e.dma_start(out=out_d[u, k * 128:(k + 1) * 128, t0 - 256:t0 - 256 + n], in_=yout[yb][:, :n])],
                          reads=yout_k[yb])
                return f
            for k in range(8):
                steps.append(s_sq(k))
            steps.append(s_rs)
            for k in range(8):
                steps.append(s_out(k))
            return steps

        def issue_w(i):
            if i < _LIMIT:
                issue_weights(i)

        P.op("pool", lambda e: e.memset(ones[:], 1.0), writes=[("ones",)])
        P.op("pool", lambda e: e.memset(epsb[:], EPS), writes=[("eps",)])
        P.op("pool", lambda e: e.memset(qsc[0:64, 0:1], 0.125), writes=[("qsc",)])
        P.op("pool", lambda e: e.memset(qsc[64:128, 0:1], 0.0), writes=[("qsc",)])
        P.op("pool", lambda e: e.memset(qsc[0:64, 1:2], 0.0), writes=[("qsc",)])
        P.op("pool", lambda e: e.memset(qsc[64:128, 1:2], 0.125), writes=[("qsc",)])
        issue_w(0)
        pi = 0
        for u in range(_NU_RUN):
            unit_inputs(u)
            for l in range(DEPTH):
                if l == 0:
                    p0_tiles = [(0, 512), (512, 512), (1024, 512)]
                    main = [(128, 512), (640, 512), (1152, 256)]
                else:
                    p0_tiles = [(128, 512), (640, 512), (1152, 256)]
                    main = [(256, 512), (768, 512)]
                gm = gmix[:, l, :]
                gf = gffn[:, l, :]
                def zero_halo(side, a0):
                    for k in range(8):
                        P.op("dve", lambda e, k=k: e.tensor_scalar(out=hT[:, k, a0:a0 + 128], in0=hT[:, k, a0:a0 + 128],
                                                                   scalar1=hm[:, side:side + 1], scalar2=None, op0=ALU.mult),
                             reads=hk([k], a0, 128) + [("hm",)], writes=hk([k], a0, 128))
                k0 = ["A", "S", "A"][:len(p0_tiles)]
                k1 = ["S", "A", "S"][:len(main)]
                first2 = "Z" if k1[-1] == "A" else "A"
                k2 = [first2 if i % 2 == 0 else ("A" if first2 == "Z" else "Z") for i in range(len(main))]
                def nsteps(tile, kind):
                    t0_, n_ = tile
                    uf_, kf_, _ = ubuf(kind, n_)
                    return norm_steps(t0_, n_, gm, uf_, kf_)
                BGN[0] = 3
                defer_w = (u == 0 and l == 0)
                ph = phases[pi]
                if not defer_w:
                    issue_w(pi + 1)
                pi += 1
                if l == 0:
                    for f_ in nsteps(p0_tiles[0], k0[0]):
                        f_()
                else:
                    bg_flush()
                for ti, (t0, n) in enumerate(p0_tiles):
                    if ti + 1 < len(p0_tiles):
                        bg_add(nsteps(p0_tiles[ti + 1], k0[ti + 1]))
                    else:
                        bg_add(nsteps(main[0], k1[0]))
                    p0_tile(l, t0, n, ph["views"](ph["base"]), ph["par"], ubuf(k0[ti], n))
                    bg_flush()
                    if defer_w and ti == 0:
                        issue_w(pi)
                BGN[0] = 3
                ph = phases[pi]; issue_w(pi + 1); pi += 1
                for ti, (t0, n) in enumerate(main):
                    if ti + 1 < len(main):
                        bg_add(nsteps(main[ti + 1], k1[ti + 1]))
                    else:
                        bg_add(nsteps(main[0], k2[0]))
                    p1_tile(l, ti, t0, n, ph["views"](ph["base"]), ph["par"], ubuf(k1[ti], n),
                            z_hazard=(ti + 1 == len(main) and k2[0] == "Z"))
                    bg_flush()
                BGN[0] = 1
                ph = phases[pi]; issue_w(pi + 1); pi += 1
                mslot = [(ti + 3) % 4 for ti in range(len(main))]
                for ti, (t0, n) in enumerate(main):
                    if ti + 1 < len(main):
                        bg_add(nsteps(main[ti + 1], k2[ti + 1]))
                    p2a_tile(ti, t0, n, mslot[ti], ph["views"](ph["base"]), ph["par"], ubuf(k2[ti], n))
                    bg_flush()
                BGN[0] = 3
                ph = phases[pi]; issue_w(pi + 1); pi += 1
                for ti, (t0, n) in enumerate(main):
                    p2b_tile(t0, n, mslot[ti], ph["views"](ph["base"]), ph["par"])
                    bg_add(p3_steps(mslot[ti], t0, n, gf), tag=("p3", ti))
                BGN[0] = 2
                for fi, nj in enumerate(FCH):
                    ph = phases[pi]; issue_w(pi + 1); pi += 1
                    last = (fi == len(FCH) - 1)
                    for ti, (t0, n) in enumerate(main):
                        if fi == 0:
                            bg_require(("p3", ti))
                        hoist = fi > 0
                        nxt = (main[ti + 1][1], mslot[ti + 1]) if (hoist and ti + 1 < len(main)) else None
                        ffn_tile(ti, t0, n, nj, ph["views"](ph["base"]), ph["par"], mslot[ti], have_g0=(hoist and ti > 0), nxt=nxt)
                        if last and l == 0:
                            if ti == 0:
                                zero_halo(0, 128)
                                t0n, nn = 128, 512
                                uf_, kf_, _ = ubuf("A", nn)
                                bg_add(norm_steps(t0n, nn, gmix[:, 1, :], uf_, kf_))
                            if ti == len(main) - 1:
                                zero_halo(1, 1280)
                        if last and l == DEPTH - 1:
                            bg_add(final_steps(u, t0, n))
                if l < DEPTH - 1:
                    pass
            bg_flush()

        P.finalize()

        with nc.Block() as block:
            @block.tensor
            def _(e):
                P.replay("pe", e, sems)

            @block.scalar
            def _(e):
                P.replay("act", e, sems)

            @block.vector
            def _(e):
                P.replay("dve", e, sems)

            @block.gpsimd
            def _(e):
                P.replay("pool", e, sems)

            @block.sync
            def _(e):
                P.replay("sp", e, sems)
                e.wait_ge(sems["outp0"], P.totals["outp0"])
                e.wait_ge(sems["outp1"], P.totals["outp1"])
    return nc


def _consts():
    k = np.arange(128)[:, None]
    q = np.arange(128)[None, :]
    slopes = np.exp2(-8.0 * np.arange(1, 9, dtype=np.float64) / 8.0)
    biasT = np.zeros((128, 6, 4, 128), np.float32)
    for hh in range(2):
        for kb in range(3):
            dist = np.abs(q - (k + (kb - 1) * 128))
            for g in range(4):
                b = -slopes[4 * hh + g] * dist
                b = np.where(dist <= 128, b, NEG)
                biasT[:, hh * 3 + kb, g, :] = b
    return biasT.reshape(128, 3072), np.eye(128, dtype=np.float32)


def _unit_meta(s_unit):
    pos = 1024 * s_unit - HALO + np.arange(TT)
    valid = (pos >= 0) & (pos < SEQ)
    maskv = np.array([float(valid[128]), float(valid[1280])], np.float32)
    kb = np.where(valid, 0.0, NEG).astype(np.float32).reshape(12, 128).T.copy()
    corr = np.ones((2, 4, 8), np.float32)
    for g, w in enumerate(POOLW):
        for side, lo in ((0, 256), (1, 1272)):
            for i in range(8):
                p = pos[lo + i]
                if 0 <= p < SEQ:
                    lo_c = min(max(p - w // 2, 0), SEQ)
                    hi_c = min(max(p + w // 2, 0), SEQ)
                    corr[side, g, i] = float(w) / float(hi_c - lo_c)
    return maskv, kb, corr.reshape(64)


def _prep(x, norm_mix, w_in, w_pool_group, pool_scale, sink, w_pool_branch, w_attn_branch, w_out,
          norm_ffn, w_ffn_gate, w_ffn_up, w_ffn_down, norm_final, cores=range(NCORES)):
    x = np.asarray(x, np.float32)
    biasT, ident = _consts()
    f = lambda a: np.ascontiguousarray(np.asarray(a, np.float32))
    pl = lambda a: np.ascontiguousarray(np.asarray(a, np.float32).reshape(a.shape[0], -1, 128).transpose(2, 0, 1).reshape(128, -1))
    shared = dict(
        biasT=biasT, ident=ident, norm_mix=pl(norm_mix), w_in=f(w_in), w_pool_group=f(w_pool_group),
        pool_scale=pl(pool_scale), sink=f(sink).reshape(-1), w_pool_branch=f(w_pool_branch),
        w_attn_branch=f(w_attn_branch), w_out=f(w_out), norm_ffn=pl(norm_ffn), w_ffn_gate=f(w_ffn_gate),
        w_ffn_up=f(w_ffn_up), w_ffn_down=f(w_ffn_down), norm_final=pl(np.asarray(norm_final)[None, :]),
    )
    in_maps = []
    for core in cores:
        xT = np.zeros((NU, D, TT), np.float32)
        maskv = np.zeros((NU, 2), np.float32)
        kb = np.zeros((NU, 128, 12), np.float32)
        corr = np.zeros((NU, 64), np.float32)
        for uu in range(NU):
            gu = core * NU + uu
            bidx, su = gu // 4, gu % 4
            lo = 1024 * su - HALO
            a, b = max(lo, 0), min(lo + TT, SEQ)
            xT[uu, :, a - lo:b - lo] = x[bidx, a:b, :].T
            maskv[uu], kb[uu], corr[uu] = _unit_meta(su)
        m = dict(shared)
        m.update(xT=xT, hmask=maskv, kbias=kb, corr=corr)
        in_maps.append(m)
    return in_maps


def kernel(**inputs):
    in_maps = _prep(**inputs)
    nc = build_nc()
    res = run_bass_kernel_spmd(nc, in_maps, core_ids=list(range(NCORES)))
    out = np.empty((NB, SEQ, D), np.float32)
    for core in range(NCORES):
        o = res.results[core]["out"]
        for uu in range(NU):
            gu = core * NU + uu
            bidx, su = gu // 4, gu % 4
            out[bidx, 1024 * su:1024 * su + 1024, :] = o[uu].T
    return out
```
